# Optimizing a Trainium2 kernel written in Bass

```python
import math
import jax, jax.numpy as jnp
from jax import lax
import numpy as np

D_MODEL = 2048
BATCH = 8
SEQ = 2048
DEPTH = 2

N_EVEN = (DEPTH + 1) // 2
N_ODD = DEPTH // 2
N_SUBLAYERS = 3
D_FF = 5632
FFN_RES_WEIGHT = 0.5
RMS_EPS = 1e-6
LN_EPS = 1e-5
NEG_INF = -1e30
MIX_WIDTH = D_MODEL
A_WIDTH = MIX_WIDTH // 2
A_HEADS = 8
A_HEAD_DIM = A_WIDTH // (2 * A_HEADS)
A_VDIM = 2 * A_HEAD_DIM
Q_BLOCK = 128
REL_BUCKETS = 32
REL_MAX_EXACT = REL_BUCKETS // 2
REL_MAX_DIST = 128
B_WIDTH = MIX_WIDTH - A_WIDTH
B_HEADS = 4
B_VDIM = B_WIDTH // B_HEADS
B_QKDIM = B_VDIM // 2
B_QK_WIDTH = B_HEADS * B_QKDIM
B_CHUNK = 64
B_CONV = 4
CONV_WIDTH = 31
COL_SIZES = (A_WIDTH, A_WIDTH, A_WIDTH, B_QK_WIDTH, B_QK_WIDTH, B_WIDTH, B_WIDTH, B_HEADS, B_HEADS)
IN_COLS = sum(COL_SIZES)
SPLIT_POINTS = tuple(int(v) for v in np.cumsum(COL_SIZES)[:-1])

kernel_name = 'hybrid_diffattn_mlstm_conformer_trunk'


def rmsnorm(x, g):
    xf = x.astype(jnp.float32)
    y = xf * lax.rsqrt(jnp.mean(xf * xf, axis=-1, keepdims=True) + RMS_EPS)
    return (y * g.astype(jnp.float32)).astype(x.dtype)


def layernorm(x, g, b):
    xf = x.astype(jnp.float32)
    mu = jnp.mean(xf, axis=-1, keepdims=True)
    var = jnp.mean(jnp.square(xf - mu), axis=-1, keepdims=True)
    y = (xf - mu) * lax.rsqrt(var + LN_EPS)
    return (y * g.astype(jnp.float32) + b.astype(jnp.float32)).astype(x.dtype)


def modulate(h, shift, scale):
    return h * (1.0 + scale[:, None, :]) + shift[:, None, :]


def swiglu(h, w1, w3, w2):
    return (jax.nn.silu(h @ w1) * (h @ w3)) @ w2


def causal_dwconv(x, w, b):
    k = w.shape[0]
    y = lax.conv_general_dilated(x, w[:, None, :], window_strides=(1,), padding=[(k - 1, 0)],
                                 dimension_numbers=('NWC', 'WIO', 'NWC'),
                                 feature_group_count=x.shape[-1])
    return y + b


def split_heads(t, n_heads):
    b, s, _ = t.shape
    return t.reshape(b, s, n_heads, -1).transpose(0, 2, 1, 3)


def t5_bias(rel_table, q_pos, k_pos):
    dist = jnp.maximum(q_pos[:, None] - k_pos[None, :], 0)
    dist_f = jnp.maximum(dist, 1).astype(jnp.float32)
    large = REL_MAX_EXACT + (jnp.log(dist_f / REL_MAX_EXACT) / math.log(REL_MAX_DIST / REL_MAX_EXACT)
                             * (REL_BUCKETS - REL_MAX_EXACT)).astype(jnp.int32)
    large = jnp.minimum(large, REL_BUCKETS - 1)
    bucket = jnp.where(dist < REL_MAX_EXACT, dist, large)
    return rel_table[bucket].transpose(2, 0, 1).astype(jnp.float32)


def diff_attention(q, k, v, lam_vecs, subln_g, rel_table, lam_init):
    b, s, _ = q.shape
    q = split_heads(q, A_HEADS)
    k = split_heads(k, A_HEADS)
    v = split_heads(v, A_HEADS)
    q1, q2 = q[..., :A_HEAD_DIM], q[..., A_HEAD_DIM:]
    k1, k2 = k[..., :A_HEAD_DIM], k[..., A_HEAD_DIM:]
    lv = lam_vecs.astype(jnp.float32)
    lam = jnp.exp(jnp.sum(lv[0] * lv[1])) - jnp.exp(jnp.sum(lv[2] * lv[3])) + lam_init
    scale = A_HEAD_DIM ** -0.5
    k_pos = jnp.arange(s)

    def block(qb):
        start = qb * Q_BLOCK
        q_pos = start + jnp.arange(Q_BLOCK)
        bias = t5_bias(rel_table, q_pos, k_pos)
        causal = k_pos[None, :] <= q_pos[:, None]

        def attn_map(qq, kk):
            qq = lax.dynamic_slice_in_dim(qq, start, Q_BLOCK, axis=2)
            logits = jnp.einsum('bhqd,bhkd->bhqk', qq, kk).astype(jnp.float32) * scale + bias
            return jax.nn.softmax(jnp.where(causal, logits, NEG_INF), axis=-1)

        a = attn_map(q1, k1) - lam * attn_map(q2, k2)
        return jnp.einsum('bhqk,bhkd->bhqd', a.astype(v.dtype), v)

    out = lax.map(block, jnp.arange(s // Q_BLOCK))
    out = out.transpose(1, 2, 0, 3, 4).reshape(b, A_HEADS, s, A_VDIM)
    out = rmsnorm(out, subln_g) * (1.0 - lam_init)
    return out.transpose(0, 2, 1, 3).reshape(b, s, A_WIDTH)


def mlstm(q, k, v, o, i_pre, f_pre, conv_w, conv_b, gate_b, norm_g):
    b, s, _ = q.shape
    qk = jax.nn.silu(causal_dwconv(jnp.concatenate([q, k], axis=-1), conv_w, conv_b))
    q, k = qk[..., :B_QK_WIDTH], qk[..., B_QK_WIDTH:]
    nc = s // B_CHUNK

    def chunks(t):
        return t.reshape(b, nc, B_CHUNK, B_HEADS, -1).transpose(1, 0, 3, 2, 4).astype(jnp.float32)

    def gchunks(t):
        return t.reshape(b, nc, B_CHUNK, B_HEADS).transpose(1, 0, 3, 2).astype(jnp.float32)

    gb = gate_b.astype(jnp.float32)
    qc = chunks(q) * (B_QKDIM ** -0.5)
    kc = chunks(k)
    vc = chunks(v)
    ic = gchunks(i_pre.astype(jnp.float32) + gb[0])
    fc = jax.nn.log_sigmoid(gchunks(f_pre.astype(jnp.float32) + gb[1]))
    tri = jnp.tril(jnp.ones((B_CHUNK, B_CHUNK), dtype=bool))

    def step(carry, inp):
        c_st, n_st, m_st = carry
        qj, kj, vj, ij, fj = inp
        bcum = jnp.cumsum(fj, axis=-1)
        dmat = bcum[..., :, None] - bcum[..., None, :] + ij[..., None, :]
        dmat = jnp.where(tri, dmat, NEG_INF)
        inter = bcum + m_st[..., None]
        m_row = jnp.maximum(inter, jnp.max(dmat, axis=-1))
        w_intra = jnp.exp(dmat - m_row[..., None])
        w_inter = jnp.exp(inter - m_row)
        sc = jnp.einsum('bhjd,bhsd->bhjs', qj, kj) * w_intra
        num = (jnp.einsum('bhjs,bhsv->bhjv', sc, vj)
               + w_inter[..., None] * jnp.einsum('bhjd,bhdv->bhjv', qj, c_st))
        den = jnp.sum(sc, axis=-1) + w_inter * jnp.einsum('bhjd,bhd->bhj', qj, n_st)
        h = num / jnp.maximum(jnp.abs(den), jnp.exp(-m_row))[..., None]
        b_last = bcum[..., -1]
        src = b_last[..., None] - bcum + ij
        m_new = jnp.maximum(b_last + m_st, jnp.max(src, axis=-1))
        w_src = jnp.exp(src - m_new[..., None])
        decay = jnp.exp(b_last + m_st - m_new)
        kw = kj * w_src[..., None]
        c_new = decay[..., None, None] * c_st + jnp.einsum('bhsd,bhsv->bhdv', kw, vj)
        n_new = decay[..., None] * n_st + jnp.sum(kw, axis=2)
        return (c_new, n_new, m_new), h

    init = (jnp.zeros((b, B_HEADS, B_QKDIM, B_VDIM), jnp.float32),
            jnp.zeros((b, B_HEADS, B_QKDIM), jnp.float32),
            jnp.zeros((b, B_HEADS), jnp.float32))
    _, hs = lax.scan(step, init, (qc, kc, vc, ic, fc))
    h = hs.transpose(1, 0, 3, 2, 4).reshape(b, s, B_HEADS, B_VDIM)
    h = rmsnorm(h, norm_g.reshape(B_HEADS, B_VDIM))
    h = h.reshape(b, s, B_WIDTH) * jax.nn.sigmoid(o.astype(jnp.float32))
    return h.astype(o.dtype)


def parallel_mixer(h, w_in, w_out, lam_vecs, subln_g, qk_conv_w, qk_conv_b, gate_b, cell_norm_g,
                   rel_table, lam_init):
    proj = h @ w_in
    qa, ka, va, qb, kb, vb, ob, ib, fb = jnp.split(proj, SPLIT_POINTS, axis=-1)
    ya = diff_attention(qa, ka, va, lam_vecs, subln_g, rel_table, lam_init)
    yb = mlstm(qb, kb, vb, ob, ib, fb, qk_conv_w, qk_conv_b, gate_b, cell_norm_g)
    return jnp.concatenate([ya, yb], axis=-1) @ w_out


def conformer_conv(h, pw1_w, pw1_b, dw_w, dw_b, ln_g, ln_b, pw2_w, pw2_b):
    u = h @ pw1_w + pw1_b
    half = u.shape[-1] // 2
    u = u[..., :half] * jax.nn.sigmoid(u[..., half:])
    u = causal_dwconv(u, dw_w, dw_b)
    u = jax.nn.silu(layernorm(u, ln_g, ln_b))
    return u @ pw2_w + pw2_b


def setup_inputs(seed: int = 0) -> dict:
    key = jax.random.key(seed)
    ks = jax.random.split(key, 32)
    f32 = jnp.float32
    D = D_MODEL

    def nrm(k, shape, scale):
        return jax.random.normal(k, shape, f32) * scale

    gate_base = jnp.stack([jnp.zeros((B_HEADS,), f32), jnp.linspace(3.0, 6.0, B_HEADS, dtype=f32)])
    return {
        'x': nrm(ks[0], (BATCH, SEQ, D), 1.0),
        'c': nrm(ks[1], (BATCH, D), 1.0),
        'mod_w': nrm(ks[2], (DEPTH, D, N_SUBLAYERS * 3 * D), 0.1 * D ** -0.5),
        'mod_b': nrm(ks[3], (DEPTH, N_SUBLAYERS * 3 * D), 0.02),
        'norm_g': 1.0 + nrm(ks[4], (DEPTH, N_SUBLAYERS, D), 0.02),
        'ffn_w1': nrm(ks[5], (DEPTH, 2, D, D_FF), D ** -0.5),
        'ffn_w3': nrm(ks[6], (DEPTH, 2, D, D_FF), D ** -0.5),
        'ffn_w2': nrm(ks[7], (DEPTH, 2, D_FF, D), D_FF ** -0.5),
        'rel_table': nrm(ks[8], (REL_BUCKETS, A_HEADS), 0.2),
        'mix_w_in': nrm(ks[9], (N_EVEN, D, IN_COLS), D ** -0.5),
        'mix_w_out': nrm(ks[10], (N_EVEN, MIX_WIDTH, D), MIX_WIDTH ** -0.5),
        'diff_lambda': nrm(ks[11], (N_EVEN, 4, A_HEAD_DIM), 0.1),
        'diff_subln_g': 1.0 + nrm(ks[12], (N_EVEN, A_VDIM), 0.02),
        'mlstm_conv_w': nrm(ks[13], (N_EVEN, B_CONV, 2 * B_QK_WIDTH), B_CONV ** -0.5),
        'mlstm_conv_b': nrm(ks[14], (N_EVEN, 2 * B_QK_WIDTH), 0.02),
        'mlstm_gate_b': gate_base[None] + nrm(ks[15], (N_EVEN, 2, B_HEADS), 0.1),
        'mlstm_norm_g': 1.0 + nrm(ks[16], (N_EVEN, B_WIDTH), 0.02),
        'conv_pw1_w': nrm(ks[17], (N_ODD, D, 2 * D), D ** -0.5),
        'conv_pw1_b': nrm(ks[18], (N_ODD, 2 * D), 0.02),
        'conv_dw_w': nrm(ks[19], (N_ODD, CONV_WIDTH, D), CONV_WIDTH ** -0.5),
        'conv_dw_b': nrm(ks[20], (N_ODD, D), 0.02),
        'conv_ln_g': 1.0 + nrm(ks[21], (N_ODD, D), 0.02),
        'conv_ln_b': nrm(ks[22], (N_ODD, D), 0.02),
        'conv_pw2_w': nrm(ks[23], (N_ODD, D, D), D ** -0.5),
        'conv_pw2_b': nrm(ks[24], (N_ODD, D), 0.02),
        'final_g': 1.0 + nrm(ks[25], (D,), 0.02),
    }


def reference(x, c, mod_w, mod_b, norm_g, ffn_w1, ffn_w3, ffn_w2, rel_table, mix_w_in, mix_w_out,
              diff_lambda, diff_subln_g, mlstm_conv_w, mlstm_conv_b, mlstm_gate_b, mlstm_norm_g,
              conv_pw1_w, conv_pw1_b, conv_dw_w, conv_dw_b, conv_ln_g, conv_ln_b, conv_pw2_w,
              conv_pw2_b, final_g):
    b = x.shape[0]
    cond = jax.nn.silu(c)
    for l in range(DEPTH):
        mod = (cond @ mod_w[l] + mod_b[l]).reshape(b, N_SUBLAYERS, 3, D_MODEL)
        h = modulate(rmsnorm(x, norm_g[l, 0]), mod[:, 0, 0], mod[:, 0, 1])
        x = x + FFN_RES_WEIGHT * (1.0 + mod[:, 0, 2])[:, None, :] * swiglu(
            h, ffn_w1[l, 0], ffn_w3[l, 0], ffn_w2[l, 0])
        h = modulate(rmsnorm(x, norm_g[l, 1]), mod[:, 1, 0], mod[:, 1, 1])
        if l % 2 == 0:
            e = l // 2
            lam_init = 0.8 - 0.6 * math.exp(-0.3 * l)
            y = parallel_mixer(h, mix_w_in[e], mix_w_out[e], diff_lambda[e], diff_subln_g[e],
                               mlstm_conv_w[e], mlstm_conv_b[e], mlstm_gate_b[e], mlstm_norm_g[e],
                               rel_table, lam_init)
        else:
            o = l // 2
            y = conformer_conv(h, conv_pw1_w[o], conv_pw1_b[o], conv_dw_w[o], conv_dw_b[o],
                               conv_ln_g[o], conv_ln_b[o], conv_pw2_w[o], conv_pw2_b[o])
        x = x + (1.0 + mod[:, 1, 2])[:, None, :] * y
        h = modulate(rmsnorm(x, norm_g[l, 2]), mod[:, 2, 0], mod[:, 2, 1])
        x = x + FFN_RES_WEIGHT * (1.0 + mod[:, 2, 2])[:, None, :] * swiglu(
            h, ffn_w1[l, 1], ffn_w3[l, 1], ffn_w2[l, 1])
    return rmsnorm(x, final_g)
```

```python
import math
from contextlib import ExitStack
import numpy as np
import concourse.bass as bass
import concourse.mybir as mybir
from concourse.bass_utils import run_bass_kernel_spmd

F32 = mybir.dt.float32
BF16 = mybir.dt.bfloat16
AF = mybir.ActivationFunctionType
ALU = mybir.AluOpType
AX = mybir.AxisListType
ENGS = ("pe", "act", "dve", "pool", "sp")

D = 2048
S = 2048
DFF = 5632
NFF = DFF // 128
INC = 6152
NEG = -30000.0
STAGES = ("ffn0a", "mix0", "ffn0b", "ffn1a", "mix1", "ffn1b")
NCORES = 8
DEBUG = False
MAXENG = "dve"
SCOPED = False
NS = 3
M0SKIP = ()
LAST = {}
USED = set()


class Tok:
    __slots__ = ("w", "r", "rd", "name", "sem", "semcnt")

    def __init__(self, name=""):
        self.w = None
        self.r = {}
        self.rd = []
        self.name = name
        self.sem = None
        self.semcnt = 0


class Op:
    __slots__ = ("eng", "fn", "deps", "sig", "dma", "semtok", "semval", "cnt", "idx", "scope")

    def __init__(self, eng, fn, dma):
        self.scope = None
        self.eng = eng
        self.fn = fn
        self.dma = dma
        self.deps = []
        self.sig = False
        self.semtok = None
        self.semval = 0
        self.cnt = 0
        self.idx = 0


class Prog:
    def __init__(self, nc):
        self.nc = nc
        self.q = {e: [] for e in ENGS}
        self.dma_toks = []
        self.nops = 0
        self.live_dma = []
        self.scope = None
        self.scoped = False

    def op(self, eng, fn, reads=(), writes=(), dma=False, semtok=None):
        o = Op(eng, fn, dma)
        o.idx = self.nops
        o.scope = self.scope
        self.nops += 1
        deps = {}

        def add(d):
            if d is None or d is o:
                return
            if (not d.dma) and d.eng == "pe" and eng == "pe":
                return
            if d.dma:
                deps[id(d)] = d
            else:
                k = d.eng
                if k not in deps or deps[k].idx < d.idx:
                    deps[k] = d

        for t in reads:
            add(t.w)
        for t in writes:
            add(t.w)
            for d in t.r.values():
                add(d)
            for d in t.rd:
                add(d)
        o.deps = list(deps.values())
        for d in o.deps:
            d.sig = True
        for t in reads:
            if dma:
                t.rd.append(o)
            else:
                t.r[eng] = o
        for t in writes:
            t.w = o
            t.r = {}
            t.rd = []
        if dma:
            st = semtok if semtok is not None else (writes[0] if writes else reads[0])
            o.semtok = st
            st.semcnt += 16
            o.semval = st.semcnt
            if st.sem is None:
                st.sem = True
                self.dma_toks.append(st)
            self.live_dma.append(o)
        self.q[eng].append(o)
        return o

    def barrier(self):
        lasts = []
        for e in ENGS:
            for o in reversed(self.q[e]):
                if not o.dma and o.fn is not None:
                    lasts.append(o)
                    break
        dm = {}
        for o in self.live_dma:
            k = id(o.semtok)
            if k not in dm or dm[k].semval < o.semval:
                dm[k] = o
        self.live_dma = []
        for e in ENGS:
            o = Op(e, None, False)
            o.idx = self.nops
            self.nops += 1
            o.deps = [d for d in lasts] + list(dm.values())
            for d in o.deps:
                d.sig = True
            self.q[e].append(o)

    def emit(self, stack):
        nc = self.nc
        engobj = {"pe": nc.tensor, "act": nc.scalar, "dve": nc.vector,
                  "pool": nc.gpsimd, "sp": nc.sync}
        esem = {e: stack.enter_context(nc.semaphore("es_" + e)) for e in ENGS}
        for i, t in enumerate(self.dma_toks):
            t.sem = stack.enter_context(nc.semaphore("ds%d" % i))
        for e in ENGS:
            c = 0
            for o in self.q[e]:
                if o.sig and not o.dma:
                    c += 1
                    o.cnt = c
        for e in ENGS:
            eng = engobj[e]
            known = {k: 0 for k in ENGS}
            knownd = {}
            cur = None
            curid = None
            for o in self.q[e]:
                if self.scoped and o.fn is not None and o.scope != cur:
                    if cur is not None:
                        nc.leave_named_scope(cur, curid, False)
                    cur = o.scope
                    curid = nc.enter_named_scope(cur, False)[0] if cur is not None else None
                for d in o.deps:
                    if d.dma:
                        k = id(d.semtok)
                        if knownd.get(k, 0) >= d.semval:
                            continue
                        eng.wait_ge(d.semtok.sem, d.semval)
                        knownd[k] = d.semval
                    else:
                        if d.eng == e and d.cnt == 0:
                            continue
                        if known[d.eng] >= d.cnt:
                            continue
                        eng.wait_ge(esem[d.eng], d.cnt)
                        known[d.eng] = d.cnt
                if o.fn is None:
                    continue
                ins = o.fn(eng)
                if o.dma:
                    ins.then_inc(o.semtok.sem, 16)
                elif o.sig:
                    ins.then_inc(esem[e], 1)
            if self.scoped and cur is not None:
                nc.leave_named_scope(cur, curid, False)


def _bucket_table():
    oh = np.zeros((33, 384), np.float32)
    for m in range(383):
        dist = 255 - m
        if dist < 0:
            oh[32, m] = 1.0
            continue
        if dist < 16:
            b = dist
        else:
            dist_f = np.float32(max(dist, 1))
            v = np.log(dist_f / np.float32(16)) / np.float32(math.log(128 / 16)) * np.float32(16)
            b = min(16 + int(np.float32(v)), 31)
        oh[b, m] += 1.0
        oh[31, m] -= 1.0
    return oh


def build():
    nc = bass.Bass("TRN2", target_bir_lowering=False)

    USED.clear()

    def din(name, shape, dt=F32, need=True):
        if not need:
            return None
        USED.add(name)
        return nc.dram_tensor(name, list(shape), dt, kind="ExternalInput").ap()

    L0 = any(k in STAGES for k in ("ffn0a", "mix0", "ffn0b"))
    L1 = any(k in STAGES for k in ("ffn1a", "mix1", "ffn1b"))
    FF = any(k.startswith("ffn") for k in STAGES)
    M0 = "mix0" in STAGES
    M1 = "mix1" in STAGES

    x_in = din("x", [S, D])
    cvec = din("cvec", [128, 16])
    vecs = din("vecs", [128, 44 * 16])
    modb = din("modb", [128, 2 * 144])
    cst = din("cst", [128, 3 * 128])
    mod_w0 = din("mod_w0", [D, 9 * D], need=L0)
    mod_w1 = din("mod_w1", [D, 9 * D], need=L1)
    mod_ws = [mod_w0, mod_w1]
    ffn_w1 = din("ffn_w1", [2, 2, D, DFF], need=FF)
    ffn_w3 = din("ffn_w3", [2, 2, D, DFF], need=FF)
    ffn_w2 = din("ffn_w2", [2, 2, DFF, D], need=FF)
    mix_w_in = din("mix_w_in", [D, INC], need=M0)
    mix_w_out = din("mix_w_out", [D, D], need=M0)
    pw1_w = din("pw1_w", [D, 2 * D], need=M1)
    pw2_w = din("pw2_w", [D, D], need=M1)
    relx = din("relx", [33, 8 * 128], need=M0)
    oh_in = din("oh", [33, 384], need=M0)
    lamrep = din("lamrep", [128, 256], need=M0)
    sublng = din("sublng", [128, 128], need=M0)
    mnormg = din("mnormg", [128, 1024], need=M0)
    gateb = din("gateb", [128, 8], need=M0)
    mconvw = din("mconvw", [128, 8 * 4], need=M0)
    mconvb = din("mconvb", [128, 8], need=M0)
    out = nc.dram_tensor("out", [S, D], F32, kind="ExternalOutput").ap()
    pt_scr = nc.dram_tensor("pt_scr", [48, 128, S], BF16, kind="ExternalOutput" if DEBUG else "Internal").ap()
    dbg_y = nc.dram_tensor("dbg_y", [16, 128, S], BF16, kind="ExternalOutput").ap() if DEBUG else None
    dbg_a = nc.dram_tensor("dbg_a", [128, 128], BF16, kind="ExternalOutput").ap() if DEBUG else None
    dbg_st = nc.dram_tensor("dbg_st", [128, 16], F32, kind="ExternalOutput").ap() if DEBUG else None
    dbg_s = nc.dram_tensor("dbg_s", [2, 128, 128], F32, kind="ExternalOutput").ap() if DEBUG else None
    dbg_tb = nc.dram_tensor("dbg_tb", [128, 256], F32, kind="ExternalOutput").ap() if DEBUG else None
    bs_scr = nc.dram_tensor("bs_scr", [8, 128, 383], F32, kind="Internal").ap()
    ya_scr = nc.dram_tensor("ya_scr", [16, 128, S], BF16, kind="Internal").ap()
    u_scr = nc.dram_tensor("u_scr", [16, 128, S + 32], BF16, kind="Internal").ap()
    y_scr = nc.dram_tensor("y_scr", [16, 128, S], BF16, kind="Internal").ap()

    st = ExitStack()
    xT = st.enter_context(nc.sbuf_tensor("xT", [128, 16, S], F32))
    WPN = 39040
    wp = st.enter_context(nc.sbuf_tensor("wp", [128, WPN], BF16))
    cf = st.enter_context(nc.sbuf_tensor("cf", [128, 128], F32))
    cb = st.enter_context(nc.sbuf_tensor("cb", [128, 3 * 128], BF16))
    vc = st.enter_context(nc.sbuf_tensor("vc", [128, 13, 16], F32))
    modv = st.enter_context(nc.sbuf_tensor("modv", [128, 2, 144], F32))
    par = st.enter_context(nc.sbuf_tensor("par", [128, 4, 16], F32))
    condT = st.enter_context(nc.sbuf_tensor("condT", [128, 16], BF16))
    ctmp = st.enter_context(nc.sbuf_tensor("ctmp", [128, 16], F32))
    psb = [st.enter_context(nc.psum_tensor("ps%d" % i, [128, 512], F32)) for i in range(8)]
    tps = [Tok("ps%d" % i) for i in range(8)]
    P = Prog(nc)
    P.scoped = SCOPED
    P.scope = "load"
    state = {"ps": 0, "off": 0, "rsv": set(), "busy": set()}
    mod_done = {}

    ident_f = cf[:, 0:128]
    ident_b = cb[:, 0:128]
    tri_b = cb[:, 128:256]
    ones_b = cb[:, 256:384]

    tx = [[Tok("x%d_%d" % (c, j)) for j in range(4)] for c in range(16)]
    t_cst, t_vc, t_modv, t_par, t_cond = Tok("cst"), Tok("vc"), Tok("modv"), Tok("par"), Tok("cond")

    def nps(hold=False):
        for _ in range(8):
            i = state["ps"]
            state["ps"] = (i + 1) % 8
            if i in state["rsv"] or i in state["busy"]:
                continue
            if hold:
                state["busy"].add(i)
            return psb[i], tps[i]
        raise AssertionError("no free PSUM bank")

    def rel(pb):
        state["busy"].discard(psb.index(pb))

    def carve_reset():
        state["off"] = 0

    def carve(nelem, dt):
        nb2 = nelem * (2 if dt == F32 else 1)
        off = state["off"]
        assert off + nb2 <= WPN, ("work pool overflow", off, nb2)
        state["off"] = off + nb2
        a = wp[:, off:off + nb2]
        if dt == F32:
            a = a.bitcast(F32)
        return a

    def dma(q, out_ap, in_ap, reads, writes, semtok=None):
        return P.op(q, lambda e: e.dma_start(out=out_ap, in_=in_ap), reads=reads, writes=writes,
                    dma=True, semtok=semtok)

    def mm(o, l, r, start, stop, reads, writes):
        return P.op("pe", lambda e: e.matmul(o, l, r, start=start, stop=stop), reads=reads, writes=writes)

    def tr(o, i, ident, reads, writes):
        return P.op("pe", lambda e: e.transpose(o, i, ident), reads=reads, writes=writes)

    def act(o, i, func, reads, writes, bias=None, scale=None, accum=None):
        kw = {}
        if bias is not None:
            kw["bias"] = bias
        if scale is not None:
            kw["scale"] = scale
        if accum is not None:
            kw["accum_out"] = accum
        return P.op("act", lambda e: e.activation(out=o, in_=i, func=func, **kw), reads=reads, writes=writes)

    def ts(eng, o, i, s1, s2, op0, op1, reads, writes):
        if s2 is None:
            return P.op(eng, lambda e: e.tensor_scalar(o, i, s1, None, op0), reads=reads, writes=writes)
        return P.op(eng, lambda e: e.tensor_scalar(o, i, s1, s2, op0, op1), reads=reads, writes=writes)

    def tt(eng, o, a, b, op, reads, writes):
        return P.op(eng, lambda e: e.tensor_tensor(o, a, b, op), reads=reads, writes=writes)

    def stt(eng, o, i0, sc, i1, op0, op1, reads, writes):
        return P.op(eng, lambda e: e.scalar_tensor_tensor(o, i0, sc, i1, op0, op1), reads=reads, writes=writes)

    def rsqrt_from(o, i, scale, eps, reads, tok):
        ts("dve", o, i, scale, eps, ALU.mult, ALU.add, reads, [tok])
        act(o, o, AF.Sqrt, [tok], [tok])
        P.op("dve", lambda e: e.reciprocal(o, o), reads=[tok], writes=[tok])

    def cp(eng, o, i, reads, writes):
        if eng == "act":
            return P.op("act", lambda e: e.activation(out=o, in_=i, func=AF.Copy), reads=reads, writes=writes)
        return P.op(eng, lambda e: e.tensor_copy(o, i), reads=reads, writes=writes)

    carve_reset()
    cst_tmp = carve(384, F32)
    t_ct = Tok("cst_tmp")
    dma("sp", cst_tmp, cst, [], [t_ct])
    dma("sp", vc[:].rearrange("p a b -> p (a b)"), vecs[:, 0:13 * 16], [], [t_vc])
    dma("sp", ctmp[:], cvec, [], [t_cond])
    cp("dve", cb[:], cst_tmp, [t_ct], [t_cst])
    cp("act", cf[:], cst_tmp[:, 0:128], [t_ct], [t_cst])
    act(condT[:], ctmp[:], AF.Silu, [t_cond], [t_cond])

    P.barrier()
    carve_reset()
    xin = [carve(2048, F32) for _ in range(2)]
    t_xin = [Tok("xin0"), Tok("xin1")]
    for i in range(16):
        b = i % 2
        dma("sp", xin[b], x_in[i * 128:(i + 1) * 128, :], [], [t_xin[b]])
        for g in range(4):
            pb, tp = nps()
            for k in range(4):
                c = g * 4 + k
                tr(pb[:, k * 128:(k + 1) * 128], xin[b][:, c * 128:(c + 1) * 128], ident_f,
                   [t_xin[b], t_cst], [tp])
            dst = xT[:, g * 4:(g + 1) * 4, i * 128:(i + 1) * 128]
            src = pb[:].rearrange("p (a b) -> p a b", a=4)
            cp("act" if (g % 2 == 0) else "dve", dst, src, [tp], [tx[g * 4 + k][i // 4] for k in range(4)])
    if not L0:
        P.barrier()

    def mod_chunks(l, ms, wb, twb, pb, tp, col0):
        n = 0
        for m in ms:
            sl = n % len(wb)
            n += 1
            dma("pool", wb[sl], mod_ws[l][:, m * 128:(m + 1) * 128].rearrange("(k p) n -> p k n", p=128),
                [], [twb[sl]])
            for k in range(16):
                mm(pb[:, col0 + m:col0 + m + 1], wb[sl][:, k, :], condT[:, k:k + 1], k == 0, k == 15,
                   [twb[sl], t_cond], [tp])
            mod_done[(l, m)] = True
            yield

    def mod_finalize(l, m0, m1, pb, tp, col0, modbt, t_mb):
        tt("dve", modv[:, l, m0:m1], pb[:, col0 + m0:col0 + m1], modbt[:, m0:m1], ALU.add, [tp, t_mb], [t_modv])

    def compute_mod(l, m0=0, m1=144, reset=True):
        ms = [m for m in range(m0, m1) if (l, m) not in mod_done]
        if not ms:
            return
        P.scope = "mod%d" % l
        if reset:
            carve_reset()
        wb = [carve(16 * 128, BF16).rearrange("p (k n) -> p k n", k=16) for _ in range(3)]
        twb = [Tok("mw%d" % i) for i in range(3)]
        modbt = carve(144, F32)
        t_mb = Tok("modbt")
        dma("sp", modbt, modb[:, l * 144:(l + 1) * 144], [], [t_mb])
        pb, tp = nps()
        rb_ = psb.index(pb)
        state["rsv"].add(rb_)
        for _ in mod_chunks(l, ms, wb, twb, pb, tp, 0):
            pass
        mod_finalize(l, ms[0], ms[-1] + 1, pb, tp, 0, modbt, t_mb)
        state["rsv"].discard(rb_)
        P.barrier()

    def set_params(l, s, ffn):
        base = s * 48
        stt("dve", par[:, 0, :], modv[:, l, base + 16:base + 32], 1.0, vc[:, l * 3 + s, :],
            ALU.add, ALU.mult, [t_modv, t_vc], [t_par])
        cp("dve", par[:, 1, :], modv[:, l, base:base + 16], [t_modv], [t_par])
        ts("dve", par[:, 2, :], modv[:, l, base + 32:base + 48], 1.0, 0.5 if ffn else 1.0,
           ALU.add, ALU.mult, [t_modv], [t_par])

    def norm_stats(j, sqb, tsq, eps, src_fn=None, src_toks=None, want_mean=False):
        pb, tp = nps()
        for c in range(16):
            b = c % 2
            act(sqb[b], xT[:, c, j * 512:(j + 1) * 512], AF.Square, [tx[c][j]], [tsq[b]])
            mm(pb[:], ones_b, sqb[b], c == 0, c == 15, [tsq[b], t_cst], [tp])
        return pb, tp

    def make_hT_gen(j, hT, t_h, sqb, tsq, rstd, t_rstd, tmpb, ttmp):
        pb, tp = nps()
        for c in range(16):
            b = c % 2
            act(sqb[b], xT[:, c, j * 512:(j + 1) * 512], AF.Square, [tx[c][j]], [tsq[b]])
            mm(pb[:], ones_b, sqb[b], c == 0, c == 15, [tsq[b], t_cst], [tp])
            yield
        ts("dve", rstd, pb[:], 1.0 / D, 1e-6, ALU.mult, ALU.add, [tp], [t_rstd])
        yield
        act(rstd, rstd, AF.Sqrt, [t_rstd], [t_rstd])
        yield
        P.op("dve", lambda e, o=rstd: e.reciprocal(o, o), reads=[t_rstd], writes=[t_rstd])
        yield
        for c in range(16):
            b = c % 2
            stt("dve", tmpb[b], xT[:, c, j * 512:(j + 1) * 512], par[:, 0, c:c + 1], rstd,
                ALU.mult, ALU.mult, [tx[c][j], t_par, t_rstd], [ttmp[b]])
            act(hT[:, c, :], tmpb[b], AF.Identity, [ttmp[b], t_par], [t_h], bias=par[:, 1, c:c + 1])
            yield

    def make_hT(j, hT, t_h, sqb, tsq, rstd, t_rstd, tmpb, ttmp):
        for _ in make_hT_gen(j, hT, t_h, sqb, tsq, rstd, t_rstd, tmpb, ttmp):
            pass

    def zip_run(gens):
        gens = list(gens)
        while gens:
            for g_ in list(gens):
                try:
                    next(g_)
                except StopIteration:
                    gens.remove(g_)

    def ffn(l, i):
        s = 0 if i == 0 else 2
        P.scope = "ffn%d%s" % (l, "ab"[i])
        set_params(l, s, True)
        carve_reset()
        hT = carve(16 * 1024, BF16).rearrange("p (c t) -> p c t", c=16)
        t_h = [Tok("hT0"), Tok("hT1")]
        sab = [carve(512, F32) for _ in range(2)]
        tsa = [Tok("sa0"), Tok("sa1")]
        w1b = [carve(2048, BF16).rearrange("p (k n) -> p k n", k=16) for _ in range(2)]
        w3b = [carve(2048, BF16).rearrange("p (k n) -> p k n", k=16) for _ in range(2)]
        NW2 = 4
        w2b = [carve(2048, BF16) for _ in range(NW2)]
        tw1 = [Tok("w1_%d" % q) for q in range(2)]
        tw3 = [Tok("w3_%d" % q) for q in range(2)]
        tw2 = [Tok("w2_%d" % q) for q in range(NW2)]
        ggraw = [carve(2 * 1024, BF16) for _ in range(2)]
        gg = [g_.rearrange("p (q t) -> p q t", q=2) for g_ in ggraw]
        tgg = [Tok("gg0"), Tok("gg1")]
        tsq2 = [[Tok("sq00"), Tok("sq01")], [Tok("sq10"), Tok("sq11")]]
        t_rstd2 = [Tok("rstd0"), Tok("rstd1")]
        W1 = ffn_w1[l, i].rearrange("(k p) n -> p k n", p=128)
        W3 = ffn_w3[l, i].rearrange("(k p) n -> p k n", p=128)
        W2 = ffn_w2[l, i].rearrange("(f p) n -> f p n", p=128)

        def w2_part(J, gi):
            gs = gi % 2
            for dc in range(16):
                for sub in range(2):
                    j = 2 * J + sub
                    pb, tp = nps()
                    for q in range(2):
                        f = 2 * gi + q
                        mm(pb[:], w2b[f % NW2][:, dc * 128:(dc + 1) * 128], gg[gs][:, q, sub * 512:(sub + 1) * 512],
                           q == 0, q == 1, [tw2[f % NW2], tgg[gs]], [tp])
                    xs = xT[:, dc, j * 512:(j + 1) * 512]
                    stt("dve", xs, pb[:], par[:, 2, dc:dc + 1], xs, ALU.mult, ALU.add,
                        [tp, t_par, tx[dc][j]], [tx[dc][j]])

        for J in range(2):
            if J > 0:
                P.barrier()
            gens = []
            for sub in range(2):
                sq_ = [ggraw[sub][:, 0:512], ggraw[sub][:, 512:1024]]
                rs_ = ggraw[sub][:, 1024:2048].bitcast(F32)
                gens.append(make_hT_gen(2 * J + sub, hT[:, :, sub * 512:(sub + 1) * 512], t_h[sub], sq_,
                                        tsq2[sub], rs_, t_rstd2[sub], [sab[sub], sab[sub]], [tsa[sub], tsa[sub]]))
            zip_run(gens)
            pend = None
            for f in range(NFF):
                ws = f % 2
                gi = f // 2
                gs = gi % 2
                dma("pool", w1b[ws], W1[:, :, f * 128:(f + 1) * 128], [], [tw1[ws]])
                dma("pool", w3b[ws], W3[:, :, f * 128:(f + 1) * 128], [], [tw3[ws]])
                dma("pool", w2b[f % NW2], W2[f], [], [tw2[f % NW2]])
                for sub in range(2):
                    hs = hT[:, :, sub * 512:(sub + 1) * 512]
                    pa, tpa = nps()
                    for k in range(16):
                        mm(pa[:], w1b[ws][:, k, :], hs[:, k, :], k == 0, k == 15, [tw1[ws], t_h[sub]], [tpa])
                    pbk, tpb = nps()
                    for k in range(16):
                        mm(pbk[:], w3b[ws][:, k, :], hs[:, k, :], k == 0, k == 15, [tw3[ws], t_h[sub]], [tpb])
                    act(sab[sub], pa[:], AF.Silu, [tpa], [tsa[sub]])
                    tt("dve", gg[gs][:, f % 2, sub * 512:(sub + 1) * 512], sab[sub], pbk[:], ALU.mult,
                       [tsa[sub], tpb], [tgg[gs]])
                if f % 2 == 1:
                    if pend is not None:
                        w2_part(J, pend)
                    pend = gi
            w2_part(J, pend)
        P.barrier()

    def final_phase():
        P.scope = "final"
        carve_reset()
        sqb = [carve(512, BF16) for _ in range(2)]
        tsq = [Tok("sq0"), Tok("sq1")]
        rstd = carve(512, F32)
        t_rstd = Tok("rstd")
        tmpb = [carve(128, F32) for _ in range(4)]
        ttmp = [Tok("ft%d" % i) for i in range(4)]
        orow = [carve(2048, F32) for _ in range(2)]
        torow = [Tok("orow0"), Tok("orow1")]
        t_out = Tok("out")
        n = 0
        for j in range(4):
            pb, tp = norm_stats(j, sqb, tsq, 1e-6)
            rsqrt_from(rstd, pb[:], 1.0 / D, 1e-6, [tp], t_rstd)
            for sub in range(4):
                ti = j * 4 + sub
                ob = ti % 2
                for g in range(4):
                    pq, tq = nps()
                    for k in range(4):
                        c = g * 4 + k
                        b = n % 4
                        n += 1
                        stt("dve", tmpb[b], xT[:, c, ti * 128:(ti + 1) * 128], vc[:, 6, c:c + 1],
                            rstd[:, sub * 128:(sub + 1) * 128], ALU.mult, ALU.mult,
                            [tx[c][j], t_vc, t_rstd], [ttmp[b]])
                        tr(pq[:, k * 128:(k + 1) * 128], tmpb[b], ident_f, [ttmp[b], t_cst], [tq])
                    cp("act", orow[ob][:, g * 512:(g + 1) * 512], pq[:], [tq], [torow[ob]])
                dma("sp", out[ti * 128:(ti + 1) * 128, :], orow[ob], [torow[ob]], [Tok("o")], semtok=torow[ob])
        P.barrier()

    def mixer1(l):
        P.scope = "m1p1"
        set_params(l, 1, False)
        tt("dve", par[:, 3, :], par[:, 2, :], vc[:, 10, :], ALU.mult, [t_par, t_vc], [t_par])
        carve_reset()
        hT = carve(16 * 512, BF16).rearrange("p (c t) -> p c t", c=16)
        t_h = Tok("hT")
        sqb = [carve(512, BF16) for _ in range(2)]
        tsq = [Tok("sq0"), Tok("sq1")]
        rstd = carve(512, F32)
        t_rstd = Tok("rstd")
        tmpb = [carve(512, F32) for _ in range(2)]
        ttmp = [Tok("tmp0"), Tok("tmp1")]
        wab = [carve(2048, BF16).rearrange("p (k n) -> p k n", k=16) for _ in range(2)]
        wgb = [carve(2048, BF16).rearrange("p (k n) -> p k n", k=16) for _ in range(2)]
        twa = [Tok("wa0"), Tok("wa1")]
        twg = [Tok("wg0"), Tok("wg1")]
        sgb = [carve(512, F32) for _ in range(2)]
        tsg = [Tok("sg0"), Tok("sg1")]
        ub = [carve(512, BF16) for _ in range(4)]
        tub = [Tok("ub%d" % i) for i in range(4)]
        zpad = carve(32, BF16)
        t_z = Tok("zpad")
        t_us = [[Tok("uscr%d_%d" % (c, q)) for q in range(5)] for c in range(16)]
        PW1 = pw1_w.rearrange("(k p) n -> p k n", p=128)
        P.op("dve", lambda e, o=zpad: e.memset(o, 0.0), writes=[t_z])
        for c in range(16):
            dma("sp", u_scr[c, :, 0:32], zpad, [t_z], [t_us[c][4]], semtok=t_z)
        n = 0
        for j in range(4):
            make_hT(j, hT, t_h, sqb, tsq, rstd, t_rstd, tmpb, ttmp)
            for c in range(16):
                fs = c % 2
                dma("pool", wab[fs], PW1[:, :, c * 128:(c + 1) * 128], [], [twa[fs]])
                dma("pool", wgb[fs], PW1[:, :, D + c * 128:D + (c + 1) * 128], [], [twg[fs]])
                pa, tpa = nps()
                for k in range(16):
                    mm(pa[:], wab[fs][:, k, :], hT[:, k, :], k == 0, k == 15, [twa[fs], t_h], [tpa])
                pg, tpg = nps()
                for k in range(16):
                    mm(pg[:], wgb[fs][:, k, :], hT[:, k, :], k == 0, k == 15, [twg[fs], t_h], [tpg])
                act(sgb[fs], pg[:], AF.Sigmoid, [tpg, t_vc], [tsg[fs]], bias=vc[:, 12, c:c + 1])
                b = n % 4
                n += 1
                stt("dve", ub[b], pa[:], vc[:, 11, c:c + 1], sgb[fs], ALU.add, ALU.mult,
                    [tpa, t_vc, tsg[fs]], [tub[b]])
                dma("sp", u_scr[c, :, 32 + j * 512:32 + (j + 1) * 512], ub[b], [tub[b]], [t_us[c][j]],
                    semtok=tub[b])
        P.barrier()
        P.scope = "m1p2"
        carve_reset()
        ubuf = [carve(S + 32, BF16) for _ in range(2)]
        tubuf = [Tok("ubuf0"), Tok("ubuf1")]
        dg = [carve(31 * 128, BF16).rearrange("p (k n) -> p k n", k=31) for _ in range(2)]
        tdg = [Tok("dg0"), Tok("dg1")]
        ybf = [carve(S, BF16) for _ in range(2)]
        tybf = [Tok("ybf0"), Tok("ybf1")]
        t_ys = [Tok("yscr%d" % c) for c in range(16)]
        taps = carve(31 * 16, F32).rearrange("p (a b) -> p a b", a=31)
        t_taps = Tok("taps")
        dma("sp", taps.rearrange("p a b -> p (a b)"), vecs[:, 13 * 16:44 * 16], [], [t_taps])
        for c in range(16):
            b = c % 2
            dma("sp", ubuf[b], u_scr[c], t_us[c], [tubuf[b]])
            for k in range(31):
                ts("dve", dg[b][:, k, :], ident_b, taps[:, k, c:c + 1], None, ALU.mult, None,
                   [t_cst, t_taps], [tdg[b]])
            for j in range(4):
                pcv, tpcv = nps()
                for k in range(31):
                    mm(pcv[:], dg[b][:, k, :], ubuf[b][:, 2 + k + j * 512:2 + k + (j + 1) * 512], k == 0, k == 30,
                       [tdg[b], tubuf[b]], [tpcv])
                act(ybf[b][:, j * 512:(j + 1) * 512], pcv[:], AF.Identity, [tpcv, t_vc], [tybf[b]],
                    bias=vc[:, 7, c:c + 1])
            dma("sp", y_scr[c], ybf[b], [tybf[b]], [t_ys[c]], semtok=tybf[b])
        P.barrier()
        P.scope = "m1p3"
        carve_reset()
        yt = carve(16 * 512, BF16).rearrange("p (c t) -> p c t", c=16)
        t_yt = Tok("yt")
        zT2 = [carve(16 * 512, BF16).rearrange("p (c t) -> p c t", c=16) for _ in range(2)]
        t_z22 = [Tok("zTa"), Tok("zTb")]
        sqb = [carve(512, BF16) for _ in range(2)]
        tsq = [Tok("sq0"), Tok("sq1")]
        mean = carve(512, F32)
        rstd = carve(512, F32)
        t_mean, t_rstd = Tok("mean"), Tok("rstd2")
        t1 = [carve(512, F32) for _ in range(2)]
        tt1 = [Tok("t1a"), Tok("t1b")]
        wpb = [carve(2048, BF16).rearrange("p (k n) -> p k n", k=16) for _ in range(2)]
        twp = [Tok("wp0"), Tok("wp1")]
        ev = [carve(512, F32) for _ in range(2)]
        tev = [Tok("ev0"), Tok("ev1")]
        PW2 = pw2_w.rearrange("(k p) n -> p k n", p=128)
        def p3_norm(j):
            zb = zT2[j % 2]
            tzb = t_z22[j % 2]
            dma("sp", yt, y_scr[:, :, j * 512:(j + 1) * 512].rearrange("c p t -> p c t"),
                [t_ys[c] for c in range(16)], [t_yt])
            pm, tpm = nps()
            for c in range(16):
                mm(pm[:], ones_b, yt[:, c, :], c == 0, c == 15, [t_yt, t_cst], [tpm])
            pq, tpq = nps()
            for c in range(16):
                b = c % 2
                act(sqb[b], yt[:, c, :], AF.Square, [t_yt], [tsq[b]])
                mm(pq[:], ones_b, sqb[b], c == 0, c == 15, [tsq[b], t_cst], [tpq])
            ts("dve", mean, pm[:], 1.0 / D, None, ALU.mult, None, [tpm], [t_mean])
            tt("dve", rstd, mean, mean, ALU.mult, [t_mean], [t_rstd])
            stt("dve", rstd, pq[:], 1.0 / D, rstd, ALU.mult, ALU.subtract, [tpq, t_rstd], [t_rstd])
            rsqrt_from(rstd, rstd, 1.0, 1e-5, [t_rstd], t_rstd)
            for c in range(16):
                b = c % 2
                tt("dve", t1[b], yt[:, c, :], mean, ALU.subtract, [t_yt, t_mean], [tt1[b]])
                tt("dve", t1[b], t1[b], rstd, ALU.mult, [tt1[b], t_rstd], [tt1[b]])
                act(zb[:, c, :], t1[b], AF.Silu, [tt1[b], t_vc], [tzb],
                    bias=vc[:, 9, c:c + 1], scale=vc[:, 8, c:c + 1])

        def p3_pw2(j):
            zb = zT2[j % 2]
            tzb = t_z22[j % 2]
            for dc in range(16):
                fs = dc % 2
                dma("pool", wpb[fs], PW2[:, :, dc * 128:(dc + 1) * 128], [], [twp[fs]])
                po, tpo = nps()
                for k in range(16):
                    mm(po[:], wpb[fs][:, k, :], zb[:, k, :], k == 0, k == 15, [twp[fs], tzb], [tpo])
                xs = xT[:, dc, j * 512:(j + 1) * 512]
                stt("dve", xs, po[:], par[:, 2, dc:dc + 1], xs, ALU.mult, ALU.add,
                    [tpo, t_par, tx[dc][j]], [tx[dc][j]])
                ts("dve", xs, xs, par[:, 3, dc:dc + 1], None, ALU.add, None, [t_par, tx[dc][j]], [tx[dc][j]])

        p3_norm(0)
        for j in range(4):
            if j + 1 < 4:
                p3_norm(j + 1)
            p3_pw2(j)
        P.barrier()

    def mixer0(l):
        lam_init = 0.8 - 0.6 * math.exp(-0.3 * l)
        P.scope = "m0p1"
        set_params(l, 1, False)
        WIN = mix_w_in.rearrange("(k p) n -> p k n", p=128)
        carve_reset()
        sm = carve(192, F32).rearrange("p (a b) -> p a b", a=3)
        gatesT = carve(S, F32)
        hT = carve(16 * 1024, BF16).rearrange("p (c t) -> p c t", c=16)
        t_h = [Tok("hT0"), Tok("hT1")]
        sqb = [carve(512, BF16) for _ in range(2)]
        tsq = [Tok("sq0"), Tok("sq1")]
        rstd = carve(512, F32)
        t_rstd = Tok("rstd")
        tmpb = [carve(512, F32) for _ in range(2)]
        ttmp = [Tok("tmp0"), Tok("tmp1")]
        wib = [carve(2048, BF16).rearrange("p (k n) -> p k n", k=16) for _ in range(2)]
        twi = [Tok("wi0"), Tok("wi1")]
        evb = [carve(512, BF16) for _ in range(4)]
        tev = [Tok("ev%d" % i) for i in range(4)]
        t_gates = Tok("gates")
        t_pt = [[Tok("pt") for _ in range(4)] for _ in range(48)]
        mwb = [carve(16 * 128, BF16).rearrange("p (k n) -> p k n", k=16) for _ in range(3)]
        tmwb = [Tok("mw%d" % i) for i in range(3)]
        mbt = [carve(144, F32) for _ in range(2)]
        t_mbt = Tok("mbt")
        dma("sp", mbt[0], modb[:, 0:144], [], [t_mbt])
        dma("sp", mbt[1], modb[:, 144:288], [], [t_mbt])
        mpb, mtp = nps()
        rbank = psb.index(mpb)
        state["rsv"].add(rbank)

        def mod_all():
            for _ in mod_chunks(0, [m for m in range(96, 144) if (0, m) not in mod_done], mwb, tmwb, mpb, mtp, 0):
                yield
            if L1:
                for _ in mod_chunks(1, [m for m in range(144) if (1, m) not in mod_done], mwb, tmwb, mpb, mtp, 144):
                    yield
        mgen = mod_all()

        def mod_step(k_=1):
            for _ in range(k_):
                try:
                    next(mgen)
                except StopIteration:
                    return
        n = 0
        for J in range(2):
            for sub in range(2):
                make_hT(2 * J + sub, hT[:, :, sub * 512:(sub + 1) * 512], t_h[sub], sqb, tsq, rstd, t_rstd, tmpb, ttmp)
            for m in range(49):
                fs = m % 2
                nco = 128 if m < 48 else 8
                dma("pool", wib[fs][:, :, 0:nco], WIN[:, :, m * 128:m * 128 + nco], [], [twi[fs]])
                for sub in range(2):
                    j = 2 * J + sub
                    hs = hT[:, :, sub * 512:(sub + 1) * 512]
                    pa, tpa = nps()
                    for k in range(16):
                        mm(pa[0:nco, :], wib[fs][:, k, 0:nco], hs[:, k, :], k == 0, k == 15, [twi[fs], t_h[sub]], [tpa])
                    if m == 48:
                        cp("act", gatesT[0:8, j * 512:(j + 1) * 512], pa[0:8, :], [tpa], [t_gates])
                        continue
                    b = n % 4
                    n += 1
                    if m < 8:
                        act(evb[b], pa[:], AF.Copy, [tpa], [tev[b]], scale=0.125)
                    elif m % 2 == 0:
                        cp("act", evb[b], pa[:], [tpa], [tev[b]])
                    else:
                        cp("dve", evb[b], pa[:], [tpa], [tev[b]])
                    dma("sp", pt_scr[m, :, j * 512:(j + 1) * 512], evb[b], [tev[b]], [t_pt[m][j]], semtok=tev[b])
                mod_step(2)
        mod_step(10 ** 6)
        mod_finalize(0, 96, 144, mpb, mtp, 0, mbt[0], t_mbt)
        if L1:
            mod_finalize(1, 0, 144, mpb, mtp, 144, mbt[1], t_mbt)
        state["rsv"].discard(rbank)
        P.barrier()

        P.scope = "m0gates"
        carve_reset()
        sm = carve(192, F32).rearrange("p (a b) -> p a b", a=3)
        gatesT = carve(S, F32)
        gt = carve(128, F32).rearrange("p (c g) -> p c g", c=16)
        fl = carve(64, F32)
        dlt = carve(64, F32)
        t_sm, t_gt = Tok("sm"), Tok("gt")
        gbt = carve(8, F32)
        t_gb = Tok("gb")
        dma("sp", gbt, gateb, [], [t_gb])
        tro = carve(256, F32)
        dma("sp", tro, cst[:, 128:384], [], [t_gb])
        tri_f = tro[:, 0:128]
        ones_f = tro[:, 128:256]
        pg, tpg = nps()
        for i in range(16):
            tr(pg[:, i * 8:(i + 1) * 8], gatesT[0:8, i * 128:(i + 1) * 128], ident_f[0:8, 0:8], [t_gates, t_cst], [tpg])
        for i in range(16):
            tt("dve", gt[:, i, :], pg[:, i * 8:(i + 1) * 8], gbt, ALU.add, [tpg, t_gb], [t_gt])
        flv = fl.rearrange("p (c h) -> p c h", c=16)
        act(flv, gt[:, :, 4:8], AF.Sigmoid, [t_gt], [t_gt])
        act(fl, fl, AF.Ln, [t_gt], [t_gt])
        pc1, tpc1 = nps()
        mm(pc1[:, 0:64], tri_f, fl, True, True, [t_gt, t_gb], [tpc1])
        pc2, tpc2 = nps()
        mm(pc2[:, 0:64], ones_f, fl, True, True, [t_gt, t_gb], [tpc2])
        act(sm[:, 0, :], pc1[:, 0:64], AF.Exp, [tpc1], [t_sm])
        tt("dve", dlt.rearrange("p (c h) -> p c h", c=16), gt[:, :, 0:4],
           pc1[:, 0:64].rearrange("p (c h) -> p c h", c=16), ALU.subtract, [t_gt, tpc1], [t_gt])
        act(sm[:, 1, :], dlt, AF.Exp, [t_gt], [t_sm])
        act(sm[:, 2, :], pc2[:, 0:64], AF.Exp, [tpc2], [t_sm])
        P.barrier()

        P.scope = "m0att"
        carve_reset()
        sm = carve(192, F32).rearrange("p (a b) -> p a b", a=3)
        qT = carve(S, BF16)
        kT = carve(S, BF16)
        vT = carve(S, BF16)
        t_q, t_k, t_v = Tok("q"), Tok("k"), Tok("v")
        yaT = vT
        t_yaT = t_v
        vtm = carve(S, BF16).rearrange("p (i d) -> p i d", i=16)
        t_vtm = Tok("vtm")
        Ssb = [[carve(S, F32) for _ in range(2)] for _ in range(2)]
        tS = [[Tok("S%d%d" % (p_, m_)) for m_ in range(2)] for p_ in range(2)]
        Abf = [carve(S, BF16) for _ in range(2)]
        t_A = [Tok("A0"), Tok("A1")]
        ATr = [carve(S, BF16) for _ in range(2)]
        AT = [a_.rearrange("p (i q) -> p i q", i=16) for a_ in ATr]
        t_AT = [Tok("AT0"), Tok("AT1")]
        TB = [carve(256, F32) for _ in range(2)]
        tTB = [Tok("TB0"), Tok("TB1")]
        yat = [carve(128, BF16) for _ in range(2)]
        t_yat = [Tok("yat0"), Tok("yat1")]
        junk = [carve(128, F32) for _ in range(2)]
        t_junk = [Tok("junk0"), Tok("junk1")]
        st8 = [carve(16, F32) for _ in range(2)]
        t_st = [Tok("st0"), Tok("st1")]
        wo = [carve(S, BF16)]
        two = [Tok("wo0")]
        sgbc = carve(128, F32)
        lamc = carve(4, F32)
        t_lam, t_sg = Tok("lam"), Tok("sg")
        rl = ATr[1][:, 0:2048].bitcast(F32)
        ohs = Abf[1][:, 0:768].bitcast(F32)
        lamt = Abf[1][:, 768:1280].bitcast(F32)
        bsb = Abf[1][:, 1280:2048].bitcast(F32)
        lj = Ssb[1][1][:, 0:128]
        t_rl, t_bsb, t_lj = Tok("rl"), Tok("bsb"), Tok("lj")
        dma("sp", rl[0:33, :], relx, [], [t_rl])
        dma("sp", ohs[0:33, :], oh_in, [], [t_rl])
        dma("sp", lamt, lamrep, [], [t_lam])
        dma("sp", sgbc, sublng, [], [t_sg])
        t_bs = [Tok("bs%d" % h) for h in range(8)]
        for h in range(8):
            pb_, tp_ = nps()
            mm(pb_[:, 0:383], rl[0:33, h * 128:(h + 1) * 128], ohs[0:33, 0:383], True, True, [t_rl], [tp_])
            cp("dve", bsb[:, 0:383], pb_[:, 0:383], [tp_], [t_bsb])
            dma("sp", bs_scr[h], bsb[:, 0:383], [t_bsb], [t_bs[h]], semtok=t_bsb)
        tt("dve", lj[:, 0:64], lamt[:, 0:64], lamt[:, 64:128], ALU.mult, [t_lam], [t_lj])
        P.op("dve", lambda e, o=lamc[:, 0:1], i=lj[:, 0:64]: e.reduce_sum(o, i, axis=AX.X), reads=[t_lj], writes=[t_lam])
        tt("dve", lj[:, 64:128], lamt[:, 128:192], lamt[:, 192:256], ALU.mult, [t_lam], [t_lj])
        P.op("dve", lambda e, o=lamc[:, 1:2], i=lj[:, 64:128]: e.reduce_sum(o, i, axis=AX.X), reads=[t_lj], writes=[t_lam])
        act(lamc[:, 0:2], lamc[:, 0:2], AF.Exp, [t_lam], [t_lam])
        stt("dve", lamc[:, 2:3], lamc[:, 1:2], -lam_init, lamc[:, 0:1], ALU.add, ALU.subtract, [t_lam], [t_lam])
        ts("dve", sgbc, sgbc, 1.0 - lam_init, None, ALU.mult, None, [t_sg], [t_sg])
        P.barrier()
        G1 = par[:, 2, :]

        t_ya = [Tok("ya%d" % i) for i in range(16)]

        def out_proj(rows0, srcT, t_src, slot):
            ci = rows0 // 128
            if DEBUG:
                dma("sp", dbg_y[ci], srcT, [t_src], [Tok("dbg")])
            dma("sp", ya_scr[ci], srcT, [t_src], [t_ya[ci]], semtok=t_src)

        s8b = [carve(8, F32) for _ in range(2)]
        t_sb = [Tok("sb0"), Tok("sb1")]

        t_stm = [[Tok("stm%d%d" % (p_, m_)) for m_ in range(2)] for p_ in range(2)]

        def att_A(h, qt, tb, ttb, mp):
            p_ = qt % 2
            L = (qt + 1) * 128
            qs = slice(qt * 128, (qt + 1) * 128)
            s8 = st8[p_]
            ts8 = t_stm[p_][mp]
            pr = slice(64 * mp, 64 * mp + 64)
            Sb = Ssb[p_][mp]
            tSb = tS[p_][mp]
            for n0 in range(0, L, 512):
                n1 = min(L, n0 + 512)
                pb_, tp_ = nps(hold=True)
                mm(pb_[:, 0:n1 - n0], qT[pr, qs], kT[pr, n0:n1], True, True, [t_q, t_k], [tp_])
                yield
                cp("act" if mp == 0 else "dve", Sb[:, n0:n1], pb_[:, 0:n1 - n0], [tp_], [tSb])
                rel(pb_)
                yield
            if qt == 0:
                tt("dve", Sb[:, 0:128], Sb[:, 0:128], tb[:, 128:256], ALU.add, [tSb, ttb], [tSb])
            else:
                tt("dve", Sb[:, L - 256:L], Sb[:, L - 256:L], tb[:, 0:256], ALU.add, [tSb, ttb], [tSb])
            yield
            P.op("dve", lambda e, o=s8[:, mp:mp + 1], i=Sb[:, 0:L]: e.reduce_max(o, i, axis=AX.X),
                 reads=[tSb], writes=[ts8])
            yield
            ts("dve", s8[:, 2 + mp:3 + mp], s8[:, mp:mp + 1], -1.0, None, ALU.mult, None, [ts8], [ts8])
            yield
            act(Sb[:, 0:L], Sb[:, 0:L], AF.Exp, [tSb, ts8], [tSb, ts8],
                bias=s8[:, 2 + mp:3 + mp], accum=s8[:, 4 + mp:5 + mp])
            yield

        def att_B1(h, qt):
            p_ = qt % 2
            L = (qt + 1) * 128
            s8 = st8[p_]
            ts8 = t_st[p_]
            S0, S1 = Ssb[p_]
            tS0, tS1 = tS[p_]
            tm0, tm1 = t_stm[p_]
            P.op("dve", lambda e, o=s8[:, 6:7], i=s8[:, 5:6]: e.reciprocal(o, i), reads=[tm1], writes=[ts8])
            yield
            stt("dve", s8[:, 8:9], s8[:, 6:7], lamc[:, 2:3], s8[:, 4:5], ALU.mult, ALU.mult, [ts8, tm0, t_lam], [ts8])
            stt("dve", s8b[p_][:, 0:1], s8[:, 4:5], 1e-6, s8[:, 4:5], ALU.mult, ALU.mult, [tm0], [t_sb[p_]])
            yield
            stt("dve", Abf[p_][:, 0:L], S1[:, 0:L], s8[:, 8:9], S0[:, 0:L], ALU.mult, ALU.add,
                [tS0, tS1, ts8], [t_A[p_]])
            yield
            for k0 in range(0, qt + 1, 8):
                k1 = min(qt + 1, k0 + 8)
                pt_, tpt_ = nps(hold=True)
                ptb = pt_[:].bitcast(BF16)
                for kt in range(k0, k1):
                    tr(ptb[:, (kt - k0) * 128:(kt - k0 + 1) * 128], Abf[p_][:, kt * 128:(kt + 1) * 128], ident_b,
                       [t_A[p_], t_cst], [tpt_])
                    if (kt - k0) % 3 == 2:
                        yield
                yield
                cp("act", AT[p_][:, k0:k1, :], ptb[:, 0:(k1 - k0) * 128].rearrange("p (i q) -> p i q", i=k1 - k0),
                   [tpt_], [t_AT[p_]])
                rel(pt_)
                yield

        def att_B2(h, qt):
            p_ = qt % 2
            qs = slice(qt * 128, (qt + 1) * 128)
            sb = s8b[p_]
            tsb = t_sb[p_]
            po_, tpo_ = nps(hold=True)
            for kt in range(qt + 1):
                mm(po_[:, 0:128], AT[p_][:, kt, :], vtm[:, kt, :], kt == 0, kt == qt, [t_AT[p_], t_vtm], [tpo_])
                if kt % 4 == 3:
                    yield
            yield
            act(junk[p_], po_[:, 0:128], AF.Square, [tpo_], [t_junk[p_], tsb], accum=sb[:, 1:2])
            yield
            act(sb[:, 2:3], sb[:, 1:2], AF.Ln, [tsb], [tsb], bias=sb[:, 0:1], scale=1.0 / 128)
            yield
            act(sb[:, 3:4], sb[:, 2:3], AF.Exp, [tsb], [tsb], scale=-0.5)
            yield
            stt("dve", yat[p_], po_[:, 0:128], sb[:, 3:4], sgbc, ALU.mult, ALU.mult, [tpo_, tsb, t_sg], [t_yat[p_]])
            rel(po_)
            yield
            py_, tpy_ = nps(hold=True)
            pyb = py_[:].bitcast(BF16)
            tr(pyb[:, 0:128], yat[p_], ident_b, [t_yat[p_], t_cst], [tpy_])
            yield
            cp("act", yaT[:, qs], pyb[:, 0:128], [tpy_], [t_yaT])
            rel(py_)
            yield

        def run_zip(gens, w=None):
            gens = list(gens)
            w = list(w) if w else [1] * len(gens)
            while gens:
                for g_, k_ in list(zip(gens, w)):
                    try:
                        for _ in range(k_):
                            next(g_)
                    except StopIteration:
                        i_ = gens.index(g_)
                        gens.pop(i_)
                        w.pop(i_)

        for h in range(0 if "att" in M0SKIP else 8):
            dma("sp", qT, pt_scr[h], t_pt[h], [t_q])
            dma("sp", kT, pt_scr[8 + h], t_pt[8 + h], [t_k])
            dma("sp", vT, pt_scr[16 + h], t_pt[16 + h], [t_v])
            tb = TB[h % 2]
            ttb = tTB[h % 2]
            skew = bass.AP(tensor=bs_scr.tensor, offset=h * 128 * 383 + 127, ap=[[382, 128], [1, 256]])
            dma("sp", tb, skew, [t_bs[h]], [ttb])
            for half in range(2):
                pv_, tpv_ = nps()
                pvb = pv_[:].bitcast(BF16)
                for i in range(8):
                    tr(pvb[:, i * 128:(i + 1) * 128], vT[:, (half * 8 + i) * 128:(half * 8 + i + 1) * 128], ident_b,
                       [t_v, t_cst], [tpv_])
                cp("dve", vtm[:, half * 8:(half + 1) * 8, :], pvb.rearrange("p (i d) -> p i d", i=8), [tpv_], [t_vtm])
            for it in range(18):
                chains = []
                wts = []
                if it < 16:
                    chains.append(att_A(h, it, tb, ttb, 0))
                    wts.append(1)
                    chains.append(att_A(h, it, tb, ttb, 1))
                    wts.append(1)
                if 0 <= it - 1 < 16:
                    chains.append(att_B1(h, it - 1))
                    wts.append(1)
                if 0 <= it - 2 < 16:
                    chains.append(att_B2(h, it - 2))
                    wts.append(1)
                run_zip(chains, wts)
            out_proj(h * 128, yaT, t_yaT, 0)
        P.barrier()

        P.scope = "m0ml"
        carve_reset()
        sm = carve(192, F32).rearrange("p (a b) -> p a b", a=3)
        qraw = carve(S + 4, BF16)
        kraw = carve(S + 4, BF16)
        t_qr, t_kr = Tok("qraw"), Tok("kraw")
        cacc = carve(S, F32)
        t_cacc = Tok("cacc")
        qc = carve(S, BF16)
        kc = carve(S, BF16)
        t_qc, t_kc = Tok("qc"), Tok("kc")
        ktm = carve(S, BF16).rearrange("p (i d) -> p i d", i=16)
        t_ktm = Tok("ktm")
        vT2 = [carve(S, BF16) for _ in range(2)]
        t_v2 = [Tok("vT2a"), Tok("vT2b")]
        v1 = carve(16 * 258, BF16).rearrange("p (i d) -> p i d", i=16)
        t_v1 = Tok("v1")
        oT = [carve(S, BF16) for _ in range(2)]
        t_oT = Tok("oT")
        Cst = carve(258, F32)
        Cb = carve(258, BF16)
        t_C, t_Cb = Tok("C"), Tok("Cb")
        Pm2 = [carve(128, BF16) for _ in range(2)]
        t_Pm2 = [Tok("Pm0"), Tok("Pm1")]
        va = [carve(258, BF16) for _ in range(2)]
        tva = [Tok("va0"), Tok("va1")]
        hh = carve(256, F32)
        t_hh = Tok("hh")
        hn = carve(256, BF16)
        t_hn = Tok("hn")
        ybT = vT2
        t_yb = t_v2
        junk = carve(256, F32)
        t_junk = Tok("junk")
        st8 = carve(16, F32)
        t_st = Tok("st")
        wo = [carve(S, BF16) for _ in range(1)]
        two = [Tok("wo0")]
        mng = carve(1024, F32)
        t_mng = Tok("mng")
        cw = carve(32, F32).rearrange("p (c k) -> p c k", c=8)
        cbv = carve(8, F32)
        t_cw = Tok("cw")
        dma("sp", mng, mnormg, [], [t_mng])
        dma("sp", cw.rearrange("p c k -> p (c k)"), mconvw, [], [t_cw])
        dma("sp", cbv, mconvb, [], [t_cw])
        epsc = carve(2, F32)[:, 0:1]
        t_eps = Tok("eps")
        P.op("dve", lambda e, o=epsc: e.memset(o, 1e-6), writes=[t_eps])
        P.op("dve", lambda e, o=qraw[:, 0:3]: e.memset(o, 0.0), writes=[t_qr])
        P.op("dve", lambda e, o=kraw[:, 0:3]: e.memset(o, 0.0), writes=[t_kr])
        P.op("dve", lambda e, o=v1[:, :, 256:257]: e.memset(o, 1.0), writes=[t_v1])

        def conv4(raw, t_raw, ci, dst, t_dst, qscale):
            ts("dve", cacc, raw[:, 0:S], cw[:, ci, 0:1], cbv[:, ci:ci + 1], ALU.mult, ALU.add,
               [t_raw, t_cw], [t_cacc])
            for k in range(1, 4):
                stt("dve", cacc, raw[:, k:k + S], cw[:, ci, k:k + 1], cacc, ALU.mult, ALU.add,
                    [t_raw, t_cw, t_cacc], [t_cacc])
            if qscale is None:
                act(dst, cacc, AF.Silu, [t_cacc], [t_dst])
            else:
                act(cacc, cacc, AF.Silu, [t_cacc], [t_cacc])
                ts("dve", dst, cacc, qscale, None, ALU.mult, None, [t_cacc], [t_dst])

        for h in range(0 if "ml" in M0SKIP else 4):
            dma("sp", qraw[:, 3:3 + S], pt_scr[24 + h], t_pt[24 + h], [t_qr])
            dma("sp", kraw[:, 3:3 + S], pt_scr[28 + h], t_pt[28 + h], [t_kr])
            for e_ in range(2):
                dma("sp", vT2[e_], pt_scr[32 + 2 * h + e_], t_pt[32 + 2 * h + e_], [t_v2[e_]])
                dma("sp", oT[e_], pt_scr[40 + 2 * h + e_], t_pt[40 + 2 * h + e_], [t_oT])
            conv4(qraw, t_qr, h, qc, t_qc, 128.0 ** -0.5)
            conv4(kraw, t_kr, 4 + h, kc, t_kc, None)
            for e_ in range(2):
                act(oT[e_], oT[e_], AF.Sigmoid, [t_oT], [t_oT])
            for half in range(2):
                pk_, tpk_ = nps()
                pkb = pk_[:].bitcast(BF16)
                for i in range(8):
                    ii = half * 8 + i
                    tr(pkb[:, i * 128:(i + 1) * 128], kc[:, ii * 128:(ii + 1) * 128], ident_b, [t_kc, t_cst], [tpk_])
                cp("dve", ktm[:, half * 8:(half + 1) * 8, :], pkb.rearrange("p (i d) -> p i d", i=8), [tpk_], [t_ktm])
                for e_ in range(2):
                    pv_, tpv_ = nps()
                    pvb = pv_[:].bitcast(BF16)
                    for i in range(8):
                        ii = half * 8 + i
                        tr(pvb[:, i * 128:(i + 1) * 128], vT2[e_][:, ii * 128:(ii + 1) * 128], ident_b,
                           [t_v2[e_], t_cst], [tpv_])
                    cp("act", v1[:, half * 8:(half + 1) * 8, e_ * 128:(e_ + 1) * 128],
                       pvb.rearrange("p (i d) -> p i d", i=8), [tpv_], [t_v1])
            P.op("dve", lambda e, o=Cst[:, 0:257]: e.memset(o, 0.0), writes=[t_C])
            P.op("dve", lambda e, o=Cb[:, 0:257]: e.memset(o, 0.0), writes=[t_Cb])
            pos = {}

            def ml_front(c):
                sl = slice(c * 128, (c + 1) * 128)
                col = c * 4 + h
                vb_ = va[c % 2]
                tvb = tva[c % 2]
                Pm_ = Pm2[c % 2]
                tPm = t_Pm2[c % 2]
                ps_, tps_ = nps()
                mm(ps_[:, 0:128], kc[:, sl], qc[:, sl], True, True, [t_kc, t_qc], [tps_])
                yield
                act(vb_[:, 0:257], v1[:, c, 0:257], AF.Copy, [t_v1, t_sm], [tvb], scale=sm[:, 1, col:col + 1])
                tt("dve", Pm_, ps_[:, 0:128], tri_b, ALU.mult, [tps_, t_cst], [tPm])
                yield
                po_, tpo_ = nps(hold=True)
                pos[c] = (po_, tpo_)
                mm(po_[:, 0:257], Pm_, vb_[:, 0:257], True, False, [tPm, tvb], [tpo_])
                mm(po_[:, 0:257], qc[:, sl], Cb[:, 0:257], False, True, [t_qc, t_Cb], [tpo_])
                yield
                pc_, tpc_ = nps()
                mm(pc_[:, 0:257], ktm[:, c, :], vb_[:, 0:257], True, True, [t_ktm, tvb], [tpc_])
                yield
                tt("dve", Cst[:, 0:257], Cst[:, 0:257], pc_[:, 0:257], ALU.add, [t_C, tpc_], [t_C])
                yield
                ts("dve", Cst[:, 0:257], Cst[:, 0:257], sm[:, 2, col:col + 1], None, ALU.mult, None, [t_C, t_sm], [t_C])
                yield
                cp("act", Cb[:, 0:257], Cst[:, 0:257], [t_C], [t_Cb])
                yield

            def ml_back(c):
                sl = slice(c * 128, (c + 1) * 128)
                col = c * 4 + h
                po_, tpo_ = pos.pop(c)
                act(st8[:, 0:1], po_[:, 256:257], AF.Abs, [tpo_], [t_st])
                yield
                ts("dve", st8[:, 0:1], st8[:, 0:1], sm[:, 0, col:col + 1], 1.0, ALU.mult, ALU.max,
                   [t_st, t_sm], [t_st])
                yield
                P.op("dve", lambda e, o=st8[:, 1:2], i=st8[:, 0:1]: e.reciprocal(o, i), reads=[t_st], writes=[t_st])
                yield
                tt("dve", st8[:, 2:3], st8[:, 1:2], sm[:, 0, col:col + 1], ALU.mult, [t_st, t_sm], [t_st])
                yield
                ts("dve", hh, po_[:, 0:256], st8[:, 2:3], None, ALU.mult, None, [tpo_, t_st], [t_hh])
                yield
                act(junk, hh, AF.Square, [t_hh], [t_junk, t_st], accum=st8[:, 3:4])
                yield
                act(st8[:, 4:5], st8[:, 3:4], AF.Ln, [t_st, t_eps], [t_st], bias=epsc, scale=1.0 / 256)
                yield
                act(st8[:, 5:6], st8[:, 4:5], AF.Exp, [t_st], [t_st], scale=-0.5)
                yield
                stt("dve", hn, hh, st8[:, 5:6], mng[:, h * 256:(h + 1) * 256], ALU.mult, ALU.mult,
                    [t_hh, t_st, t_mng], [t_hn])
                yield
                for e_ in range(2):
                    py_, tpy_ = nps()
                    pyb = py_[:].bitcast(BF16)
                    tr(pyb[:, 0:128], hn[:, e_ * 128:(e_ + 1) * 128], ident_b, [t_hn, t_cst], [tpy_])
                    yield
                    tt("dve", ybT[e_][:, sl], pyb[:, 0:128], oT[e_][:, sl], ALU.mult, [tpy_, t_oT], [t_yb[e_]])
                    if e_ == 1:
                        rel(po_)
                    yield

            def run_zip2(gens):
                gens = list(gens)
                while gens:
                    for g_ in list(gens):
                        try:
                            next(g_)
                        except StopIteration:
                            gens.remove(g_)

            for it in range(17):
                chains = []
                if it < 16:
                    chains.append(ml_front(it))
                if it >= 1:
                    chains.append(ml_back(it - 1))
                run_zip2(chains)
            for e_ in range(2):
                out_proj(1024 + h * 256 + e_ * 128, ybT[e_], t_yb[e_], 0)
        P.barrier()

        P.scope = "m0out"
        carve_reset()
        yy = [carve(16 * 512, BF16).rearrange("p (c t) -> p c t", c=16) for _ in range(2)]
        t_yy = [Tok("yy0"), Tok("yy1")]
        wob = [carve(2048, BF16).rearrange("p (k n) -> p k n", k=16) for _ in range(3)]
        twob = [Tok("wob%d" % i) for i in range(3)]
        WO = mix_w_out.rearrange("(k p) n -> p k n", p=128)
        nw = 0
        for j in range(4):
            yb_ = yy[j % 2]
            dma("sp", yb_, ya_scr[:, :, j * 512:(j + 1) * 512].rearrange("c p t -> p c t"),
                [t_ya[ci] for ci in range(16) if t_ya[ci].w is not None], [t_yy[j % 2]])
            for dc in range(16):
                ws_ = nw % 3
                nw += 1
                dma("pool", wob[ws_], WO[:, :, dc * 128:(dc + 1) * 128], [], [twob[ws_]])
                po_, tpo_ = nps()
                for k in range(16):
                    mm(po_[:], wob[ws_][:, k, :], yb_[:, k, :], k == 0, k == 15, [twob[ws_], t_yy[j % 2]], [tpo_])
                xs = xT[:, dc, j * 512:(j + 1) * 512]
                stt("dve", xs, po_[:], par[:, 2, dc:dc + 1], xs, ALU.mult, ALU.add,
                    [tpo_, t_par, tx[dc][j]], [tx[dc][j]])
        P.barrier()

    if L0:
        compute_mod(0, 0, 96, reset=False)
        if "ffn0a" in STAGES:
            ffn(0, 0)
        if "mix0" in STAGES:
            mixer0(0)
        compute_mod(0, 96, 144)
        if "ffn0b" in STAGES:
            ffn(0, 1)
    if L1:
        compute_mod(1)
        if "ffn1a" in STAGES:
            ffn(1, 0)
        if "mix1" in STAGES:
            mixer1(1)
        if "ffn1b" in STAGES:
            ffn(1, 1)
    final_phase()
    P.emit(st)
    st.close()
    return nc


_NC = None


def _prep_inputs(inp):
    f32 = np.float32
    g = lambda k: np.ascontiguousarray(np.asarray(inp[k], dtype=f32))
    vlist = []
    ng = g("norm_g")
    for l in range(2):
        for s in range(3):
            vlist.append(ng[l, s])
    vlist.append(g("final_g"))
    vlist.append(g("conv_dw_b")[0])
    vlist.append(g("conv_ln_g")[0])
    vlist.append(g("conv_ln_b")[0])
    vlist.append(g("conv_pw2_b")[0])
    pb = g("conv_pw1_b")[0]
    vlist.append(pb[:D])
    vlist.append(pb[D:])
    dw = g("conv_dw_w")[0]
    for k in range(31):
        vlist.append(dw[k])
    vecs = np.stack([v.reshape(16, 128).T for v in vlist], axis=1)
    vecs = np.ascontiguousarray(vecs.reshape(128, 44 * 16))
    mb = g("mod_b")
    modb = np.ascontiguousarray(np.stack([mb[l].reshape(144, 128).T for l in range(2)], axis=1).reshape(128, 288))
    cst = np.zeros((128, 384), f32)
    cst[:, 0:128] = np.eye(128, dtype=f32)
    cst[:, 128:256] = np.triu(np.ones((128, 128), f32))
    cst[:, 256:384] = 1.0
    rel = g("rel_table")
    relx = np.zeros((33, 8, 128), f32)
    relx[:32] = rel[:, :, None]
    relx[32] = NEG
    lam = g("diff_lambda")[0].reshape(1, 256)
    mcw = g("mlstm_conv_w")[0]
    mconvw = np.ascontiguousarray(mcw.reshape(4, 8, 128).transpose(2, 1, 0).reshape(128, 32))
    mconvb = np.ascontiguousarray(g("mlstm_conv_b")[0].reshape(8, 128).T)
    USED.update(("x", "cvec"))
    shared = {
        "vecs": vecs, "modb": modb, "cst": cst,
        "mod_w0": g("mod_w")[0], "mod_w1": g("mod_w")[1], "ffn_w1": g("ffn_w1"), "ffn_w3": g("ffn_w3"), "ffn_w2": g("ffn_w2"),
        "mix_w_in": g("mix_w_in")[0], "mix_w_out": g("mix_w_out")[0],
        "pw1_w": g("conv_pw1_w")[0], "pw2_w": g("conv_pw2_w")[0],
        "relx": np.ascontiguousarray(relx.reshape(33, 1024)), "oh": _bucket_table(),
        "lamrep": np.ascontiguousarray(np.broadcast_to(lam, (128, 256))),
        "sublng": np.ascontiguousarray(np.broadcast_to(g("diff_subln_g")[0][None, :], (128, 128))),
        "mnormg": np.ascontiguousarray(np.broadcast_to(g("mlstm_norm_g")[0][None, :], (128, 1024))),
        "gateb": np.ascontiguousarray(np.broadcast_to(g("mlstm_gate_b")[0].reshape(1, 8), (128, 8))),
        "mconvw": mconvw, "mconvb": mconvb,
    }
    x = g("x")
    c = g("c")
    maps = []
    for b in range(NCORES):
        m = dict(shared)
        m["x"] = x[b]
        m["cvec"] = np.ascontiguousarray(c[b].reshape(16, 128).T)
        maps.append(m)
    return maps


def kernel(**inputs):
    global _NC
    nc = build()
    maps = _prep_inputs(inputs)
    maps = [{k: v for k, v in m.items() if k in USED} for m in maps]
    res = run_bass_kernel_spmd(nc, maps, core_ids=list(range(NCORES)))
    if DEBUG:
        LAST.update(res.results[0])
    return np.stack([r["out"] for r in res.results], axis=0).astype(np.float32)
```

```python
import math
from contextlib import ExitStack
import numpy as np
import concourse.bass as bass
import concourse.mybir as mybir
from concourse.bass_utils import run_bass_kernel_spmd

F32 = mybir.dt.float32
BF16 = mybir.dt.bfloat16
AF = mybir.ActivationFunctionType
ALU = mybir.AluOpType
AX = mybir.AxisListType
ENGS = ("pe", "act", "dve", "pool", "sp")

D = 2048
S = 2048
DFF = 5632
NFF = DFF // 128
INC = 6152
NEG = -30000.0
STAGES = ("ffn0a", "mix0", "ffn0b", "ffn1a", "mix1", "ffn1b")
NCORES = 8
DEBUG = False
MAXENG = "dve"
SCOPED = False
NS = 3
M0SKIP = ()
LAST = {}
USED = set()


class Tok:
    __slots__ = ("w", "r", "rd", "name", "sem", "semcnt")

    def __init__(self, name=""):
        self.w = None
        self.r = {}
        self.rd = []
        self.name = name
        self.sem = None
        self.semcnt = 0


class Op:
    __slots__ = ("eng", "fn", "deps", "sig", "dma", "semtok", "semval", "cnt", "idx", "scope")

    def __init__(self, eng, fn, dma):
        self.scope = None
        self.eng = eng
        self.fn = fn
        self.dma = dma
        self.deps = []
        self.sig = False
        self.semtok = None
        self.semval = 0
        self.cnt = 0
        self.idx = 0


class Prog:
    def __init__(self, nc):
        self.nc = nc
        self.q = {e: [] for e in ENGS}
        self.dma_toks = []
        self.nops = 0
        self.live_dma = []
        self.scope = None
        self.scoped = False

    def op(self, eng, fn, reads=(), writes=(), dma=False, semtok=None):
        o = Op(eng, fn, dma)
        o.idx = self.nops
        o.scope = self.scope
        self.nops += 1
        deps = {}

        def add(d):
            if d is None or d is o:
                return
            if (not d.dma) and d.eng == "pe" and eng == "pe":
                return
            if d.dma:
                deps[id(d)] = d
            else:
                k = d.eng
                if k not in deps or deps[k].idx < d.idx:
                    deps[k] = d

        for t in reads:
            add(t.w)
        for t in writes:
            add(t.w)
            for d in t.r.values():
                add(d)
            for d in t.rd:
                add(d)
        o.deps = list(deps.values())
        for d in o.deps:
            d.sig = True
        for t in reads:
            if dma:
                t.rd.append(o)
            else:
                t.r[eng] = o
        for t in writes:
            t.w = o
            t.r = {}
            t.rd = []
        if dma:
            st = semtok if semtok is not None else (writes[0] if writes else reads[0])
            o.semtok = st
            st.semcnt += 16
            o.semval = st.semcnt
            if st.sem is None:
                st.sem = True
                self.dma_toks.append(st)
            self.live_dma.append(o)
        self.q[eng].append(o)
        return o

    def barrier(self):
        lasts = []
        for e in ENGS:
            for o in reversed(self.q[e]):
                if not o.dma and o.fn is not None:
                    lasts.append(o)
                    break
        dm = {}
        for o in self.live_dma:
            k = id(o.semtok)
            if k not in dm or dm[k].semval < o.semval:
                dm[k] = o
        self.live_dma = []
        for e in ENGS:
            o = Op(e, None, False)
            o.idx = self.nops
            self.nops += 1
            o.deps = [d for d in lasts] + list(dm.values())
            for d in o.deps:
                d.sig = True
            self.q[e].append(o)

    def emit(self, stack):
        nc = self.nc
        engobj = {"pe": nc.tensor, "act": nc.scalar, "dve": nc.vector,
                  "pool": nc.gpsimd, "sp": nc.sync}
        esem = {e: stack.enter_context(nc.semaphore("es_" + e)) for e in ENGS}
        for i, t in enumerate(self.dma_toks):
            t.sem = stack.enter_context(nc.semaphore("ds%d" % i))
        for e in ENGS:
            c = 0
            for o in self.q[e]:
                if o.sig and not o.dma:
                    c += 1
                    o.cnt = c
        for e in ENGS:
            eng = engobj[e]
            known = {k: 0 for k in ENGS}
            knownd = {}
            cur = None
            curid = None
            for o in self.q[e]:
                if self.scoped and o.fn is not None and o.scope != cur:
                    if cur is not None:
                        nc.leave_named_scope(cur, curid, False)
                    cur = o.scope
                    curid = nc.enter_named_scope(cur, False)[0] if cur is not None else None
                for d in o.deps:
                    if d.dma:
                        k = id(d.semtok)
                        if knownd.get(k, 0) >= d.semval:
                            continue
                        eng.wait_ge(d.semtok.sem, d.semval)
                        knownd[k] = d.semval
                    else:
                        if d.eng == e and d.cnt == 0:
                            continue
                        if known[d.eng] >= d.cnt:
                            continue
                        eng.wait_ge(esem[d.eng], d.cnt)
                        known[d.eng] = d.cnt
                if o.fn is None:
                    continue
                ins = o.fn(eng)
                if o.dma:
                    ins.then_inc(o.semtok.sem, 16)
                elif o.sig:
                    ins.then_inc(esem[e], 1)
            if self.scoped and cur is not None:
                nc.leave_named_scope(cur, curid, False)


def _bucket_table():
    oh = np.zeros((33, 384), np.float32)
    for m in range(383):
        dist = 255 - m
        if dist < 0:
            oh[32, m] = 1.0
            continue
        if dist < 16:
            b = dist
        else:
            dist_f = np.float32(max(dist, 1))
            v = np.log(dist_f / np.float32(16)) / np.float32(math.log(128 / 16)) * np.float32(16)
            b = min(16 + int(np.float32(v)), 31)
        oh[b, m] += 1.0
        oh[31, m] -= 1.0
    return oh


def build():
    nc = bass.Bass("TRN2", target_bir_lowering=False)

    USED.clear()

    def din(name, shape, dt=F32, need=True):
        if not need:
            return None
        USED.add(name)
        return nc.dram_tensor(name, list(shape), dt, kind="ExternalInput").ap()

    L0 = any(k in STAGES for k in ("ffn0a", "mix0", "ffn0b"))
    L1 = any(k in STAGES for k in ("ffn1a", "mix1", "ffn1b"))
    FF = any(k.startswith("ffn") for k in STAGES)
    M0 = "mix0" in STAGES
    M1 = "mix1" in STAGES

    x_in = din("x", [S, D])
    cvec = din("cvec", [128, 16])
    vecs = din("vecs", [128, 44 * 16])
    modb = din("modb", [128, 2 * 144])
    cst = din("cst", [128, 3 * 128])
    mod_w0 = din("mod_w0", [D, 9 * D], need=L0)
    mod_w1 = din("mod_w1", [D, 9 * D], need=L1)
    mod_ws = [mod_w0, mod_w1]
    ffn_w1 = din("ffn_w1", [2, 2, D, DFF], need=FF)
    ffn_w3 = din("ffn_w3", [2, 2, D, DFF], need=FF)
    ffn_w2 = din("ffn_w2", [2, 2, DFF, D], need=FF)
    mix_w_in = din("mix_w_in", [D, INC], need=M0)
    mix_w_out = din("mix_w_out", [D, D], need=M0)
    pw1_w = din("pw1_w", [D, 2 * D], need=M1)
    pw2_w = din("pw2_w", [D, D], need=M1)
    relx = din("relx", [33, 8 * 128], need=M0)
    oh_in = din("oh", [33, 384], need=M0)
    lamrep = din("lamrep", [128, 256], need=M0)
    sublng = din("sublng", [128, 128], need=M0)
    mnormg = din("mnormg", [128, 1024], need=M0)
    gateb = din("gateb", [128, 8], need=M0)
    mconvw = din("mconvw", [128, 8 * 4], need=M0)
    mconvb = din("mconvb", [128, 8], need=M0)
    out = nc.dram_tensor("out", [S, D], F32, kind="ExternalOutput").ap()
    pt_scr = nc.dram_tensor("pt_scr", [48, 128, S], BF16, kind="ExternalOutput" if DEBUG else "Internal").ap()
    dbg_y = nc.dram_tensor("dbg_y", [16, 128, S], BF16, kind="ExternalOutput").ap() if DEBUG else None
    dbg_a = nc.dram_tensor("dbg_a", [128, 128], BF16, kind="ExternalOutput").ap() if DEBUG else None
    dbg_st = nc.dram_tensor("dbg_st", [128, 16], F32, kind="ExternalOutput").ap() if DEBUG else None
    dbg_s = nc.dram_tensor("dbg_s", [2, 128, 128], F32, kind="ExternalOutput").ap() if DEBUG else None
    dbg_tb = nc.dram_tensor("dbg_tb", [128, 256], F32, kind="ExternalOutput").ap() if DEBUG else None
    bs_scr = nc.dram_tensor("bs_scr", [8, 128, 383], F32, kind="Internal").ap()
    ya_scr = nc.dram_tensor("ya_scr", [16, 128, S], BF16, kind="Internal").ap()
    u_scr = nc.dram_tensor("u_scr", [16, 128, S + 32], BF16, kind="Internal").ap()
    y_scr = nc.dram_tensor("y_scr", [16, 128, S], BF16, kind="Internal").ap()

    st = ExitStack()
    xT = st.enter_context(nc.sbuf_tensor("xT", [128, 16, S], F32))
    WPN = 39040
    wp = st.enter_context(nc.sbuf_tensor("wp", [128, WPN], BF16))
    cf = st.enter_context(nc.sbuf_tensor("cf", [128, 128], F32))
    cb = st.enter_context(nc.sbuf_tensor("cb", [128, 3 * 128], BF16))
    vc = st.enter_context(nc.sbuf_tensor("vc", [128, 13, 16], F32))
    modv = st.enter_context(nc.sbuf_tensor("modv", [128, 2, 144], F32))
    par = st.enter_context(nc.sbuf_tensor("par", [128, 4, 16], F32))
    condT = st.enter_context(nc.sbuf_tensor("condT", [128, 16], BF16))
    ctmp = st.enter_context(nc.sbuf_tensor("ctmp", [128, 16], F32))
    psb = [st.enter_context(nc.psum_tensor("ps%d" % i, [128, 512], F32)) for i in range(8)]
    tps = [Tok("ps%d" % i) for i in range(8)]
    P = Prog(nc)
    P.scoped = SCOPED
    P.scope = "load"
    state = {"ps": 0, "off": 0, "rsv": set(), "busy": set()}
    mod_done = {}

    ident_f = cf[:, 0:128]
    ident_b = cb[:, 0:128]
    tri_b = cb[:, 128:256]
    ones_b = cb[:, 256:384]

    tx = [[Tok("x%d_%d" % (c, j)) for j in range(4)] for c in range(16)]
    t_cst, t_vc, t_modv, t_par, t_cond = Tok("cst"), Tok("vc"), Tok("modv"), Tok("par"), Tok("cond")

    def nps(hold=False):
        for _ in range(8):
            i = state["ps"]
            state["ps"] = (i + 1) % 8
            if i in state["rsv"] or i in state["busy"]:
                continue
            if hold:
                state["busy"].add(i)
            return psb[i], tps[i]
        raise AssertionError("no free PSUM bank")

    def rel(pb):
        state["busy"].discard(psb.index(pb))

    def carve_reset():
        state["off"] = 0

    def carve(nelem, dt):
        nb2 = nelem * (2 if dt == F32 else 1)
        off = state["off"]
        assert off + nb2 <= WPN, ("work pool overflow", off, nb2)
        state["off"] = off + nb2
        a = wp[:, off:off + nb2]
        if dt == F32:
            a = a.bitcast(F32)
        return a

    def dma(q, out_ap, in_ap, reads, writes, semtok=None):
        return P.op(q, lambda e: e.dma_start(out=out_ap, in_=in_ap), reads=reads, writes=writes,
                    dma=True, semtok=semtok)

    def mm(o, l, r, start, stop, reads, writes):
        return P.op("pe", lambda e: e.matmul(o, l, r, start=start, stop=stop), reads=reads, writes=writes)

    def tr(o, i, ident, reads, writes):
        return P.op("pe", lambda e: e.transpose(o, i, ident), reads=reads, writes=writes)

    def act(o, i, func, reads, writes, bias=None, scale=None, accum=None):
        kw = {}
        if bias is not None:
            kw["bias"] = bias
        if scale is not None:
            kw["scale"] = scale
        if accum is not None:
            kw["accum_out"] = accum
        return P.op("act", lambda e: e.activation(out=o, in_=i, func=func, **kw), reads=reads, writes=writes)

    def ts(eng, o, i, s1, s2, op0, op1, reads, writes):
        if s2 is None:
            return P.op(eng, lambda e: e.tensor_scalar(o, i, s1, None, op0), reads=reads, writes=writes)
        return P.op(eng, lambda e: e.tensor_scalar(o, i, s1, s2, op0, op1), reads=reads, writes=writes)

    def tt(eng, o, a, b, op, reads, writes):
        return P.op(eng, lambda e: e.tensor_tensor(o, a, b, op), reads=reads, writes=writes)

    def stt(eng, o, i0, sc, i1, op0, op1, reads, writes):
        return P.op(eng, lambda e: e.scalar_tensor_tensor(o, i0, sc, i1, op0, op1), reads=reads, writes=writes)

    def rsqrt_from(o, i, scale, eps, reads, tok):
        ts("dve", o, i, scale, eps, ALU.mult, ALU.add, reads, [tok])
        act(o, o, AF.Sqrt, [tok], [tok])
        P.op("dve", lambda e: e.reciprocal(o, o), reads=[tok], writes=[tok])

    def cp(eng, o, i, reads, writes):
        if eng == "act":
            return P.op("act", lambda e: e.activation(out=o, in_=i, func=AF.Copy), reads=reads, writes=writes)
        return P.op(eng, lambda e: e.tensor_copy(o, i), reads=reads, writes=writes)

    carve_reset()
    cst_tmp = carve(384, F32)
    t_ct = Tok("cst_tmp")
    dma("sp", cst_tmp, cst, [], [t_ct])
    dma("sp", vc[:].rearrange("p a b -> p (a b)"), vecs[:, 0:13 * 16], [], [t_vc])
    dma("sp", ctmp[:], cvec, [], [t_cond])
    cp("dve", cb[:], cst_tmp, [t_ct], [t_cst])
    cp("act", cf[:], cst_tmp[:, 0:128], [t_ct], [t_cst])
    act(condT[:], ctmp[:], AF.Silu, [t_cond], [t_cond])

    P.barrier()
    carve_reset()
    xin = [carve(2048, F32) for _ in range(2)]
    t_xin = [Tok("xin0"), Tok("xin1")]
    for i in range(16):
        b = i % 2
        dma("sp", xin[b], x_in[i * 128:(i + 1) * 128, :], [], [t_xin[b]])
        for g in range(4):
            pb, tp = nps()
            for k in range(4):
                c = g * 4 + k
                tr(pb[:, k * 128:(k + 1) * 128], xin[b][:, c * 128:(c + 1) * 128], ident_f,
                   [t_xin[b], t_cst], [tp])
            dst = xT[:, g * 4:(g + 1) * 4, i * 128:(i + 1) * 128]
            src = pb[:].rearrange("p (a b) -> p a b", a=4)
            cp("act" if (g % 2 == 0) else "dve", dst, src, [tp], [tx[g * 4 + k][i // 4] for k in range(4)])
    if not L0:
        P.barrier()

    def mod_chunks(l, ms, wb, twb, pb, tp, col0):
        n = 0
        for m in ms:
            sl = n % len(wb)
            n += 1
            dma("pool", wb[sl], mod_ws[l][:, m * 128:(m + 1) * 128].rearrange("(k p) n -> p k n", p=128),
                [], [twb[sl]])
            for k in range(16):
                mm(pb[:, col0 + m:col0 + m + 1], wb[sl][:, k, :], condT[:, k:k + 1], k == 0, k == 15,
                   [twb[sl], t_cond], [tp])
            mod_done[(l, m)] = True
            yield

    def mod_finalize(l, m0, m1, pb, tp, col0, modbt, t_mb):
        tt("dve", modv[:, l, m0:m1], pb[:, col0 + m0:col0 + m1], modbt[:, m0:m1], ALU.add, [tp, t_mb], [t_modv])

    def compute_mod(l, m0=0, m1=144, reset=True):
        ms = [m for m in range(m0, m1) if (l, m) not in mod_done]
        if not ms:
            return
        P.scope = "mod%d" % l
        if reset:
            carve_reset()
        wb = [carve(16 * 128, BF16).rearrange("p (k n) -> p k n", k=16) for _ in range(3)]
        twb = [Tok("mw%d" % i) for i in range(3)]
        modbt = carve(144, F32)
        t_mb = Tok("modbt")
        dma("sp", modbt, modb[:, l * 144:(l + 1) * 144], [], [t_mb])
        pb, tp = nps()
        rb_ = psb.index(pb)
        state["rsv"].add(rb_)
        for _ in mod_chunks(l, ms, wb, twb, pb, tp, 0):
            pass
        mod_finalize(l, ms[0], ms[-1] + 1, pb, tp, 0, modbt, t_mb)
        state["rsv"].discard(rb_)
        P.barrier()

    def set_params(l, s, ffn):
        base = s * 48
        stt("dve", par[:, 0, :], modv[:, l, base + 16:base + 32], 1.0, vc[:, l * 3 + s, :],
            ALU.add, ALU.mult, [t_modv, t_vc], [t_par])
        cp("dve", par[:, 1, :], modv[:, l, base:base + 16], [t_modv], [t_par])
        ts("dve", par[:, 2, :], modv[:, l, base + 32:base + 48], 1.0, 0.5 if ffn else 1.0,
           ALU.add, ALU.mult, [t_modv], [t_par])

    def norm_stats(j, sqb, tsq, eps, src_fn=None, src_toks=None, want_mean=False):
        pb, tp = nps()
        for c in range(16):
            b = c % 2
            act(sqb[b], xT[:, c, j * 512:(j + 1) * 512], AF.Square, [tx[c][j]], [tsq[b]])
            mm(pb[:], ones_b, sqb[b], c == 0, c == 15, [tsq[b], t_cst], [tp])
        return pb, tp

    def make_hT_gen(j, hT, t_h, sqb, tsq, rstd, t_rstd, tmpb, ttmp):
        pb, tp = nps()
        for c in range(16):
            b = c % 2
            act(sqb[b], xT[:, c, j * 512:(j + 1) * 512], AF.Square, [tx[c][j]], [tsq[b]])
            mm(pb[:], ones_b, sqb[b], c == 0, c == 15, [tsq[b], t_cst], [tp])
            yield
        ts("dve", rstd, pb[:], 1.0 / D, 1e-6, ALU.mult, ALU.add, [tp], [t_rstd])
        yield
        act(rstd, rstd, AF.Sqrt, [t_rstd], [t_rstd])
        yield
        P.op("dve", lambda e, o=rstd: e.reciprocal(o, o), reads=[t_rstd], writes=[t_rstd])
        yield
        for c in range(16):
            b = c % 2
            stt("dve", tmpb[b], xT[:, c, j * 512:(j + 1) * 512], par[:, 0, c:c + 1], rstd,
                ALU.mult, ALU.mult, [tx[c][j], t_par, t_rstd], [ttmp[b]])
            act(hT[:, c, :], tmpb[b], AF.Identity, [ttmp[b], t_par], [t_h], bias=par[:, 1, c:c + 1])
            yield

    def make_hT(j, hT, t_h, sqb, tsq, rstd, t_rstd, tmpb, ttmp):
        for _ in make_hT_gen(j, hT, t_h, sqb, tsq, rstd, t_rstd, tmpb, ttmp):
            pass

    def zip_run(gens):
        gens = list(gens)
        while gens:
            for g_ in list(gens):
                try:
                    next(g_)
                except StopIteration:
                    gens.remove(g_)

    def ffn(l, i):
        s = 0 if i == 0 else 2
        P.scope = "ffn%d%s" % (l, "ab"[i])
        set_params(l, s, True)
        carve_reset()
        hT = carve(16 * 1024, BF16).rearrange("p (c t) -> p c t", c=16)
        t_h = [Tok("hT0"), Tok("hT1")]
        sab = [carve(512, F32) for _ in range(2)]
        tsa = [Tok("sa0"), Tok("sa1")]
        w1b = [carve(2048, BF16).rearrange("p (k n) -> p k n", k=16) for _ in range(2)]
        w3b = [carve(2048, BF16).rearrange("p (k n) -> p k n", k=16) for _ in range(2)]
        NW2 = 4
        w2b = [carve(2048, BF16) for _ in range(NW2)]
        tw1 = [Tok("w1_%d" % q) for q in range(2)]
        tw3 = [Tok("w3_%d" % q) for q in range(2)]
        tw2 = [Tok("w2_%d" % q) for q in range(NW2)]
        ggraw = [carve(2 * 1024, BF16) for _ in range(2)]
        gg = [g_.rearrange("p (q t) -> p q t", q=2) for g_ in ggraw]
        tgg = [Tok("gg0"), Tok("gg1")]
        tsq2 = [[Tok("sq00"), Tok("sq01")], [Tok("sq10"), Tok("sq11")]]
        t_rstd2 = [Tok("rstd0"), Tok("rstd1")]
        W1 = ffn_w1[l, i].rearrange("(k p) n -> p k n", p=128)
        W3 = ffn_w3[l, i].rearrange("(k p) n -> p k n", p=128)
        W2 = ffn_w2[l, i].rearrange("(f p) n -> f p n", p=128)

        def w2_part(J, gi):
            gs = gi % 2
            for dc in range(16):
                for sub in range(2):
                    j = 2 * J + sub
                    pb, tp = nps()
                    for q in range(2):
                        f = 2 * gi + q
                        mm(pb[:], w2b[f % NW2][:, dc * 128:(dc + 1) * 128], gg[gs][:, q, sub * 512:(sub + 1) * 512],
                           q == 0, q == 1, [tw2[f % NW2], tgg[gs]], [tp])
                    xs = xT[:, dc, j * 512:(j + 1) * 512]
                    stt("dve", xs, pb[:], par[:, 2, dc:dc + 1], xs, ALU.mult, ALU.add,
                        [tp, t_par, tx[dc][j]], [tx[dc][j]])

        for J in range(2):
            if J > 0:
                P.barrier()
            gens = []
            for sub in range(2):
                sq_ = [ggraw[sub][:, 0:512], ggraw[sub][:, 512:1024]]
                rs_ = ggraw[sub][:, 1024:2048].bitcast(F32)
                gens.append(make_hT_gen(2 * J + sub, hT[:, :, sub * 512:(sub + 1) * 512], t_h[sub], sq_,
                                        tsq2[sub], rs_, t_rstd2[sub], [sab[sub], sab[sub]], [tsa[sub], tsa[sub]]))
            zip_run(gens)
            pend = None
            for f in range(NFF):
                ws = f % 2
                gi = f // 2
                gs = gi % 2
                dma("pool", w1b[ws], W1[:, :, f * 128:(f + 1) * 128], [], [tw1[ws]])
                dma("pool", w3b[ws], W3[:, :, f * 128:(f + 1) * 128], [], [tw3[ws]])
                dma("pool", w2b[f % NW2], W2[f], [], [tw2[f % NW2]])
                for sub in range(2):
                    hs = hT[:, :, sub * 512:(sub + 1) * 512]
                    pa, tpa = nps()
                    for k in range(16):
                        mm(pa[:], w1b[ws][:, k, :], hs[:, k, :], k == 0, k == 15, [tw1[ws], t_h[sub]], [tpa])
                    pbk, tpb = nps()
                    for k in range(16):
                        mm(pbk[:], w3b[ws][:, k, :], hs[:, k, :], k == 0, k == 15, [tw3[ws], t_h[sub]], [tpb])
                    act(sab[sub], pa[:], AF.Silu, [tpa], [tsa[sub]])
                    tt("dve", gg[gs][:, f % 2, sub * 512:(sub + 1) * 512], sab[sub], pbk[:], ALU.mult,
                       [tsa[sub], tpb], [tgg[gs]])
                if f % 2 == 1:
                    if pend is not None:
                        w2_part(J, pend)
                    pend = gi
            w2_part(J, pend)
        P.barrier()

    def final_phase():
        P.scope = "final"
        carve_reset()
        sqb = [carve(512, BF16) for _ in range(2)]
        tsq = [Tok("sq0"), Tok("sq1")]
        rstd = carve(512, F32)
        t_rstd = Tok("rstd")
        tmpb = [carve(128, F32) for _ in range(4)]
        ttmp = [Tok("ft%d" % i) for i in range(4)]
        orow = [carve(2048, F32) for _ in range(2)]
        torow = [Tok("orow0"), Tok("orow1")]
        t_out = Tok("out")
        n = 0
        for j in range(4):
            pb, tp = norm_stats(j, sqb, tsq, 1e-6)
            rsqrt_from(rstd, pb[:], 1.0 / D, 1e-6, [tp], t_rstd)
            for sub in range(4):
                ti = j * 4 + sub
                ob = ti % 2
                for g in range(4):
                    pq, tq = nps()
                    for k in range(4):
                        c = g * 4 + k
                        b = n % 4
                        n += 1
                        stt("dve", tmpb[b], xT[:, c, ti * 128:(ti + 1) * 128], vc[:, 6, c:c + 1],
                            rstd[:, sub * 128:(sub + 1) * 128], ALU.mult, ALU.mult,
                            [tx[c][j], t_vc, t_rstd], [ttmp[b]])
                        tr(pq[:, k * 128:(k + 1) * 128], tmpb[b], ident_f, [ttmp[b], t_cst], [tq])
                    cp("act", orow[ob][:, g * 512:(g + 1) * 512], pq[:], [tq], [torow[ob]])
                dma("sp", out[ti * 128:(ti + 1) * 128, :], orow[ob], [torow[ob]], [Tok("o")], semtok=torow[ob])
        P.barrier()

    def mixer1(l):
        P.scope = "m1p1"
        set_params(l, 1, False)
        tt("dve", par[:, 3, :], par[:, 2, :], vc[:, 10, :], ALU.mult, [t_par, t_vc], [t_par])
        carve_reset()
        hT = carve(16 * 512, BF16).rearrange("p (c t) -> p c t", c=16)
        t_h = Tok("hT")
        sqb = [carve(512, BF16) for _ in range(2)]
        tsq = [Tok("sq0"), Tok("sq1")]
        rstd = carve(512, F32)
        t_rstd = Tok("rstd")
        tmpb = [carve(512, F32) for _ in range(2)]
        ttmp = [Tok("tmp0"), Tok("tmp1")]
        wab = [carve(2048, BF16).rearrange("p (k n) -> p k n", k=16) for _ in range(2)]
        wgb = [carve(2048, BF16).rearrange("p (k n) -> p k n", k=16) for _ in range(2)]
        twa = [Tok("wa0"), Tok("wa1")]
        twg = [Tok("wg0"), Tok("wg1")]
        sgb = [carve(512, F32) for _ in range(2)]
        tsg = [Tok("sg0"), Tok("sg1")]
        ub = [carve(512, BF16) for _ in range(4)]
        tub = [Tok("ub%d" % i) for i in range(4)]
        zpad = carve(32, BF16)
        t_z = Tok("zpad")
        t_us = [[Tok("uscr%d_%d" % (c, q)) for q in range(5)] for c in range(16)]
        PW1 = pw1_w.rearrange("(k p) n -> p k n", p=128)
        P.op("dve", lambda e, o=zpad: e.memset(o, 0.0), writes=[t_z])
        for c in range(16):
            dma("sp", u_scr[c, :, 0:32], zpad, [t_z], [t_us[c][4]], semtok=t_z)
        n = 0
        for j in range(4):
            make_hT(j, hT, t_h, sqb, tsq, rstd, t_rstd, tmpb, ttmp)
            for c in range(16):
                fs = c % 2
                dma("pool", wab[fs], PW1[:, :, c * 128:(c + 1) * 128], [], [twa[fs]])
                dma("pool", wgb[fs], PW1[:, :, D + c * 128:D + (c + 1) * 128], [], [twg[fs]])
                pa, tpa = nps()
                for k in range(16):
                    mm(pa[:], wab[fs][:, k, :], hT[:, k, :], k == 0, k == 15, [twa[fs], t_h], [tpa])
                pg, tpg = nps()
                for k in range(16):
                    mm(pg[:], wgb[fs][:, k, :], hT[:, k, :], k == 0, k == 15, [twg[fs], t_h], [tpg])
                act(sgb[fs], pg[:], AF.Sigmoid, [tpg, t_vc], [tsg[fs]], bias=vc[:, 12, c:c + 1])
                b = n % 4
                n += 1
                stt("dve", ub[b], pa[:], vc[:, 11, c:c + 1], sgb[fs], ALU.add, ALU.mult,
                    [tpa, t_vc, tsg[fs]], [tub[b]])
                dma("sp", u_scr[c, :, 32 + j * 512:32 + (j + 1) * 512], ub[b], [tub[b]], [t_us[c][j]],
                    semtok=tub[b])
        P.barrier()
        P.scope = "m1p2"
        carve_reset()
        ubuf = [carve(S + 32, BF16) for _ in range(2)]
        tubuf = [Tok("ubuf0"), Tok("ubuf1")]
        dg = [carve(31 * 128, BF16).rearrange("p (k n) -> p k n", k=31) for _ in range(2)]
        tdg = [Tok("dg0"), Tok("dg1")]
        ybf = [carve(S, BF16) for _ in range(2)]
        tybf = [Tok("ybf0"), Tok("ybf1")]
        t_ys = [Tok("yscr%d" % c) for c in range(16)]
        taps = carve(31 * 16, F32).rearrange("p (a b) -> p a b", a=31)
        t_taps = Tok("taps")
        dma("sp", taps.rearrange("p a b -> p (a b)"), vecs[:, 13 * 16:44 * 16], [], [t_taps])
        for c in range(16):
            b = c % 2
            dma("sp", ubuf[b], u_scr[c], t_us[c], [tubuf[b]])
            for k in range(31):
                ts("dve", dg[b][:, k, :], ident_b, taps[:, k, c:c + 1], None, ALU.mult, None,
                   [t_cst, t_taps], [tdg[b]])
            for j in range(4):
                pcv, tpcv = nps()
                for k in range(31):
                    mm(pcv[:], dg[b][:, k, :], ubuf[b][:, 2 + k + j * 512:2 + k + (j + 1) * 512], k == 0, k == 30,
                       [tdg[b], tubuf[b]], [tpcv])
                act(ybf[b][:, j * 512:(j + 1) * 512], pcv[:], AF.Identity, [tpcv, t_vc], [tybf[b]],
                    bias=vc[:, 7, c:c + 1])
            dma("sp", y_scr[c], ybf[b], [tybf[b]], [t_ys[c]], semtok=tybf[b])
        P.barrier()
        P.scope = "m1p3"
        carve_reset()
        yt = carve(16 * 512, BF16).rearrange("p (c t) -> p c t", c=16)
        t_yt = Tok("yt")
        zT2 = [carve(16 * 512, BF16).rearrange("p (c t) -> p c t", c=16) for _ in range(2)]
        t_z22 = [Tok("zTa"), Tok("zTb")]
        sqb = [carve(512, BF16) for _ in range(2)]
        tsq = [Tok("sq0"), Tok("sq1")]
        mean = carve(512, F32)
        rstd = carve(512, F32)
        t_mean, t_rstd = Tok("mean"), Tok("rstd2")
        t1 = [carve(512, F32) for _ in range(2)]
        tt1 = [Tok("t1a"), Tok("t1b")]
        wpb = [carve(2048, BF16).rearrange("p (k n) -> p k n", k=16) for _ in range(2)]
        twp = [Tok("wp0"), Tok("wp1")]
        ev = [carve(512, F32) for _ in range(2)]
        tev = [Tok("ev0"), Tok("ev1")]
        PW2 = pw2_w.rearrange("(k p) n -> p k n", p=128)
        def p3_norm(j):
            zb = zT2[j % 2]
            tzb = t_z22[j % 2]
            dma("sp", yt, y_scr[:, :, j * 512:(j + 1) * 512].rearrange("c p t -> p c t"),
                [t_ys[c] for c in range(16)], [t_yt])
            pm, tpm = nps()
            for c in range(16):
                mm(pm[:], ones_b, yt[:, c, :], c == 0, c == 15, [t_yt, t_cst], [tpm])
            pq, tpq = nps()
            for c in range(16):
                b = c % 2
                act(sqb[b], yt[:, c, :], AF.Square, [t_yt], [tsq[b]])
                mm(pq[:], ones_b, sqb[b], c == 0, c == 15, [tsq[b], t_cst], [tpq])
            ts("dve", mean, pm[:], 1.0 / D, None, ALU.mult, None, [tpm], [t_mean])
            tt("dve", rstd, mean, mean, ALU.mult, [t_mean], [t_rstd])
            stt("dve", rstd, pq[:], 1.0 / D, rstd, ALU.mult, ALU.subtract, [tpq, t_rstd], [t_rstd])
            rsqrt_from(rstd, rstd, 1.0, 1e-5, [t_rstd], t_rstd)
            for c in range(16):
                b = c % 2
                tt("dve", t1[b], yt[:, c, :], mean, ALU.subtract, [t_yt, t_mean], [tt1[b]])
                tt("dve", t1[b], t1[b], rstd, ALU.mult, [tt1[b], t_rstd], [tt1[b]])
                act(zb[:, c, :], t1[b], AF.Silu, [tt1[b], t_vc], [tzb],
                    bias=vc[:, 9, c:c + 1], scale=vc[:, 8, c:c + 1])

        def p3_pw2(j):
            zb = zT2[j % 2]
            tzb = t_z22[j % 2]
            for dc in range(16):
                fs = dc % 2
                dma("pool", wpb[fs], PW2[:, :, dc * 128:(dc + 1) * 128], [], [twp[fs]])
                po, tpo = nps()
                for k in range(16):
                    mm(po[:], wpb[fs][:, k, :], zb[:, k, :], k == 0, k == 15, [twp[fs], tzb], [tpo])
                xs = xT[:, dc, j * 512:(j + 1) * 512]
                stt("dve", xs, po[:], par[:, 2, dc:dc + 1], xs, ALU.mult, ALU.add,
                    [tpo, t_par, tx[dc][j]], [tx[dc][j]])
                ts("dve", xs, xs, par[:, 3, dc:dc + 1], None, ALU.add, None, [t_par, tx[dc][j]], [tx[dc][j]])

        p3_norm(0)
        for j in range(4):
            if j + 1 < 4:
                p3_norm(j + 1)
            p3_pw2(j)
        P.barrier()

    def mixer0(l):
        lam_init = 0.8 - 0.6 * math.exp(-0.3 * l)
        P.scope = "m0p1"
        set_params(l, 1, False)
        WIN = mix_w_in.rearrange("(k p) n -> p k n", p=128)
        carve_reset()
        sm = carve(192, F32).rearrange("p (a b) -> p a b", a=3)
        gatesT = carve(S, F32)
        hT = carve(16 * 1024, BF16).rearrange("p (c t) -> p c t", c=16)
        t_h = [Tok("hT0"), Tok("hT1")]
        sqb = [carve(512, BF16) for _ in range(2)]
        tsq = [Tok("sq0"), Tok("sq1")]
        rstd = carve(512, F32)
        t_rstd = Tok("rstd")
        tmpb = [carve(512, F32) for _ in range(2)]
        ttmp = [Tok("tmp0"), Tok("tmp1")]
        wib = [carve(2048, BF16).rearrange("p (k n) -> p k n", k=16) for _ in range(2)]
        twi = [Tok("wi0"), Tok("wi1")]
        evb = [carve(512, BF16) for _ in range(4)]
        tev = [Tok("ev%d" % i) for i in range(4)]
        t_gates = Tok("gates")
        t_pt = [[Tok("pt") for _ in range(4)] for _ in range(48)]
        mwb = [carve(16 * 128, BF16).rearrange("p (k n) -> p k n", k=16) for _ in range(3)]
        tmwb = [Tok("mw%d" % i) for i in range(3)]
        mbt = [carve(144, F32) for _ in range(2)]
        t_mbt = Tok("mbt")
        dma("sp", mbt[0], modb[:, 0:144], [], [t_mbt])
        dma("sp", mbt[1], modb[:, 144:288], [], [t_mbt])
        mpb, mtp = nps()
        rbank = psb.index(mpb)
        state["rsv"].add(rbank)

        def mod_all():
            for _ in mod_chunks(0, [m for m in range(96, 144) if (0, m) not in mod_done], mwb, tmwb, mpb, mtp, 0):
                yield
            if L1:
                for _ in mod_chunks(1, [m for m in range(144) if (1, m) not in mod_done], mwb, tmwb, mpb, mtp, 144):
                    yield
        mgen = mod_all()

        def mod_step(k_=1):
            for _ in range(k_):
                try:
                    next(mgen)
                except StopIteration:
                    return
        n = 0
        for J in range(2):
            for sub in range(2):
                make_hT(2 * J + sub, hT[:, :, sub * 512:(sub + 1) * 512], t_h[sub], sqb, tsq, rstd, t_rstd, tmpb, ttmp)
            for m in range(49):
                fs = m % 2
                nco = 128 if m < 48 else 8
                dma("pool", wib[fs][:, :, 0:nco], WIN[:, :, m * 128:m * 128 + nco], [], [twi[fs]])
                for sub in range(2):
                    j = 2 * J + sub
                    hs = hT[:, :, sub * 512:(sub + 1) * 512]
                    pa, tpa = nps()
                    for k in range(16):
                        mm(pa[0:nco, :], wib[fs][:, k, 0:nco], hs[:, k, :], k == 0, k == 15, [twi[fs], t_h[sub]], [tpa])
                    if m == 48:
                        cp("act", gatesT[0:8, j * 512:(j + 1) * 512], pa[0:8, :], [tpa], [t_gates])
                        continue
                    b = n % 4
                    n += 1
                    if m < 8:
                        act(evb[b], pa[:], AF.Copy, [tpa], [tev[b]], scale=0.125)
                    elif m % 2 == 0:
                        cp("act", evb[b], pa[:], [tpa], [tev[b]])
                    else:
                        cp("dve", evb[b], pa[:], [tpa], [tev[b]])
                    dma("sp", pt_scr[m, :, j * 512:(j + 1) * 512], evb[b], [tev[b]], [t_pt[m][j]], semtok=tev[b])
                mod_step(2)
        mod_step(10 ** 6)
        mod_finalize(0, 96, 144, mpb, mtp, 0, mbt[0], t_mbt)
        if L1:
            mod_finalize(1, 0, 144, mpb, mtp, 144, mbt[1], t_mbt)
        state["rsv"].discard(rbank)
        P.barrier()

        P.scope = "m0gates"
        carve_reset()
        sm = carve(192, F32).rearrange("p (a b) -> p a b", a=3)
        gatesT = carve(S, F32)
        gt = carve(128, F32).rearrange("p (c g) -> p c g", c=16)
        fl = carve(64, F32)
        dlt = carve(64, F32)
        t_sm, t_gt = Tok("sm"), Tok("gt")
        gbt = carve(8, F32)
        t_gb = Tok("gb")
        dma("sp", gbt, gateb, [], [t_gb])
        tro = carve(256, F32)
        dma("sp", tro, cst[:, 128:384], [], [t_gb])
        tri_f = tro[:, 0:128]
        ones_f = tro[:, 128:256]
        pg, tpg = nps()
        for i in range(16):
            tr(pg[:, i * 8:(i + 1) * 8], gatesT[0:8, i * 128:(i + 1) * 128], ident_f[0:8, 0:8], [t_gates, t_cst], [tpg])
        for i in range(16):
            tt("dve", gt[:, i, :], pg[:, i * 8:(i + 1) * 8], gbt, ALU.add, [tpg, t_gb], [t_gt])
        flv = fl.rearrange("p (c h) -> p c h", c=16)
        act(flv, gt[:, :, 4:8], AF.Sigmoid, [t_gt], [t_gt])
        act(fl, fl, AF.Ln, [t_gt], [t_gt])
        pc1, tpc1 = nps()
        mm(pc1[:, 0:64], tri_f, fl, True, True, [t_gt, t_gb], [tpc1])
        pc2, tpc2 = nps()
        mm(pc2[:, 0:64], ones_f, fl, True, True, [t_gt, t_gb], [tpc2])
        act(sm[:, 0, :], pc1[:, 0:64], AF.Exp, [tpc1], [t_sm])
        tt("dve", dlt.rearrange("p (c h) -> p c h", c=16), gt[:, :, 0:4],
           pc1[:, 0:64].rearrange("p (c h) -> p c h", c=16), ALU.subtract, [t_gt, tpc1], [t_gt])
        act(sm[:, 1, :], dlt, AF.Exp, [t_gt], [t_sm])
        act(sm[:, 2, :], pc2[:, 0:64], AF.Exp, [tpc2], [t_sm])
        P.barrier()

        P.scope = "m0att"
        carve_reset()
        sm = carve(192, F32).rearrange("p (a b) -> p a b", a=3)
        qT = carve(S, BF16)
        kT = carve(S, BF16)
        vT = carve(S, BF16)
        t_q, t_k, t_v = Tok("q"), Tok("k"), Tok("v")
        yaT = vT
        t_yaT = t_v
        vtm = carve(S, BF16).rearrange("p (i d) -> p i d", i=16)
        t_vtm = Tok("vtm")
        Ssb = [[carve(S, F32) for _ in range(2)] for _ in range(2)]
        tS = [[Tok("S%d%d" % (p_, m_)) for m_ in range(2)] for p_ in range(2)]
        Abf = [carve(S, BF16) for _ in range(2)]
        t_A = [Tok("A0"), Tok("A1")]
        ATr = [carve(S, BF16) for _ in range(2)]
        AT = [a_.rearrange("p (i q) -> p i q", i=16) for a_ in ATr]
        t_AT = [Tok("AT0"), Tok("AT1")]
        TB = [carve(256, F32) for _ in range(2)]
        tTB = [Tok("TB0"), Tok("TB1")]
        yat = [carve(128, BF16) for _ in range(2)]
        t_yat = [Tok("yat0"), Tok("yat1")]
        junk = [carve(128, F32) for _ in range(2)]
        t_junk = [Tok("junk0"), Tok("junk1")]
        st8 = [carve(16, F32) for _ in range(2)]
        t_st = [Tok("st0"), Tok("st1")]
        wo = [carve(S, BF16)]
        two = [Tok("wo0")]
        sgbc = carve(128, F32)
        lamc = carve(4, F32)
        t_lam, t_sg = Tok("lam"), Tok("sg")
        rl = ATr[1][:, 0:2048].bitcast(F32)
        ohs = Abf[1][:, 0:768].bitcast(F32)
        lamt = Abf[1][:, 768:1280].bitcast(F32)
        bsb = Abf[1][:, 1280:2048].bitcast(F32)
        lj = Ssb[1][1][:, 0:128]
        t_rl, t_bsb, t_lj = Tok("rl"), Tok("bsb"), Tok("lj")
        dma("sp", rl[0:33, :], relx, [], [t_rl])
        dma("sp", ohs[0:33, :], oh_in, [], [t_rl])
        dma("sp", lamt, lamrep, [], [t_lam])
        dma("sp", sgbc, sublng, [], [t_sg])
        t_bs = [Tok("bs%d" % h) for h in range(8)]
        for h in range(8):
            pb_, tp_ = nps()
            mm(pb_[:, 0:383], rl[0:33, h * 128:(h + 1) * 128], ohs[0:33, 0:383], True, True, [t_rl], [tp_])
            cp("dve", bsb[:, 0:383], pb_[:, 0:383], [tp_], [t_bsb])
            dma("sp", bs_scr[h], bsb[:, 0:383], [t_bsb], [t_bs[h]], semtok=t_bsb)
        tt("dve", lj[:, 0:64], lamt[:, 0:64], lamt[:, 64:128], ALU.mult, [t_lam], [t_lj])
        P.op("dve", lambda e, o=lamc[:, 0:1], i=lj[:, 0:64]: e.reduce_sum(o, i, axis=AX.X), reads=[t_lj], writes=[t_lam])
        tt("dve", lj[:, 64:128], lamt[:, 128:192], lamt[:, 192:256], ALU.mult, [t_lam], [t_lj])
        P.op("dve", lambda e, o=lamc[:, 1:2], i=lj[:, 64:128]: e.reduce_sum(o, i, axis=AX.X), reads=[t_lj], writes=[t_lam])
        act(lamc[:, 0:2], lamc[:, 0:2], AF.Exp, [t_lam], [t_lam])
        stt("dve", lamc[:, 2:3], lamc[:, 1:2], -lam_init, lamc[:, 0:1], ALU.add, ALU.subtract, [t_lam], [t_lam])
        ts("dve", sgbc, sgbc, 1.0 - lam_init, None, ALU.mult, None, [t_sg], [t_sg])
        P.barrier()
        G1 = par[:, 2, :]

        t_ya = [Tok("ya%d" % i) for i in range(16)]

        def out_proj(rows0, srcT, t_src, slot):
            ci = rows0 // 128
            if DEBUG:
                dma("sp", dbg_y[ci], srcT, [t_src], [Tok("dbg")])
            dma("sp", ya_scr[ci], srcT, [t_src], [t_ya[ci]], semtok=t_src)

        s8b = [carve(8, F32) for _ in range(2)]
        t_sb = [Tok("sb0"), Tok("sb1")]

        t_stm = [[Tok("stm%d%d" % (p_, m_)) for m_ in range(2)] for p_ in range(2)]

        def att_A(h, qt, tb, ttb, mp):
            p_ = qt % 2
            L = (qt + 1) * 128
            qs = slice(qt * 128, (qt + 1) * 128)
            s8 = st8[p_]
            ts8 = t_stm[p_][mp]
            pr = slice(64 * mp, 64 * mp + 64)
            Sb = Ssb[p_][mp]
            tSb = tS[p_][mp]
            for n0 in range(0, L, 512):
                n1 = min(L, n0 + 512)
                pb_, tp_ = nps(hold=True)
                mm(pb_[:, 0:n1 - n0], qT[pr, qs], kT[pr, n0:n1], True, True, [t_q, t_k], [tp_])
                yield
                cp("act" if mp == 0 else "dve", Sb[:, n0:n1], pb_[:, 0:n1 - n0], [tp_], [tSb])
                rel(pb_)
                yield
            if qt == 0:
                tt("dve", Sb[:, 0:128], Sb[:, 0:128], tb[:, 128:256], ALU.add, [tSb, ttb], [tSb])
            else:
                tt("dve", Sb[:, L - 256:L], Sb[:, L - 256:L], tb[:, 0:256], ALU.add, [tSb, ttb], [tSb])
            yield
            P.op("dve", lambda e, o=s8[:, mp:mp + 1], i=Sb[:, 0:L]: e.reduce_max(o, i, axis=AX.X),
                 reads=[tSb], writes=[ts8])
            yield
            ts("dve", s8[:, 2 + mp:3 + mp], s8[:, mp:mp + 1], -1.0, None, ALU.mult, None, [ts8], [ts8])
            yield
            act(Sb[:, 0:L], Sb[:, 0:L], AF.Exp, [tSb, ts8], [tSb, ts8],
                bias=s8[:, 2 + mp:3 + mp], accum=s8[:, 4 + mp:5 + mp])
            yield

        def att_B1(h, qt):
            p_ = qt % 2
            L = (qt + 1) * 128
            s8 = st8[p_]
            ts8 = t_st[p_]
            S0, S1 = Ssb[p_]
            tS0, tS1 = tS[p_]
            tm0, tm1 = t_stm[p_]
            P.op("dve", lambda e, o=s8[:, 6:7], i=s8[:, 5:6]: e.reciprocal(o, i), reads=[tm1], writes=[ts8])
            yield
            stt("dve", s8[:, 8:9], s8[:, 6:7], lamc[:, 2:3], s8[:, 4:5], ALU.mult, ALU.mult, [ts8, tm0, t_lam], [ts8])
            stt("dve", s8b[p_][:, 0:1], s8[:, 4:5], 1e-6, s8[:, 4:5], ALU.mult, ALU.mult, [tm0], [t_sb[p_]])
            yield
            stt("dve", Abf[p_][:, 0:L], S1[:, 0:L], s8[:, 8:9], S0[:, 0:L], ALU.mult, ALU.add,
                [tS0, tS1, ts8], [t_A[p_]])
            yield
            for k0 in range(0, qt + 1, 8):
                k1 = min(qt + 1, k0 + 8)
                pt_, tpt_ = nps(hold=True)
                ptb = pt_[:].bitcast(BF16)
                for kt in range(k0, k1):
                    tr(ptb[:, (kt - k0) * 128:(kt - k0 + 1) * 128], Abf[p_][:, kt * 128:(kt + 1) * 128], ident_b,
                       [t_A[p_], t_cst], [tpt_])
                    if (kt - k0) % 3 == 2:
                        yield
                yield
                cp("act", AT[p_][:, k0:k1, :], ptb[:, 0:(k1 - k0) * 128].rearrange("p (i q) -> p i q", i=k1 - k0),
                   [tpt_], [t_AT[p_]])
                rel(pt_)
                yield

        def att_B2(h, qt):
            p_ = qt % 2
            qs = slice(qt * 128, (qt + 1) * 128)
            sb = s8b[p_]
            tsb = t_sb[p_]
            po_, tpo_ = nps(hold=True)
            for kt in range(qt + 1):
                mm(po_[:, 0:128], AT[p_][:, kt, :], vtm[:, kt, :], kt == 0, kt == qt, [t_AT[p_], t_vtm], [tpo_])
                if kt % 4 == 3:
                    yield
            yield
            act(junk[p_], po_[:, 0:128], AF.Square, [tpo_], [t_junk[p_], tsb], accum=sb[:, 1:2])
            yield
            act(sb[:, 2:3], sb[:, 1:2], AF.Ln, [tsb], [tsb], bias=sb[:, 0:1], scale=1.0 / 128)
            yield
            act(sb[:, 3:4], sb[:, 2:3], AF.Exp, [tsb], [tsb], scale=-0.5)
            yield
            stt("dve", yat[p_], po_[:, 0:128], sb[:, 3:4], sgbc, ALU.mult, ALU.mult, [tpo_, tsb, t_sg], [t_yat[p_]])
            rel(po_)
            yield
            py_, tpy_ = nps(hold=True)
            pyb = py_[:].bitcast(BF16)
            tr(pyb[:, 0:128], yat[p_], ident_b, [t_yat[p_], t_cst], [tpy_])
            yield
            cp("act", yaT[:, qs], pyb[:, 0:128], [tpy_], [t_yaT])
            rel(py_)
            yield

        def run_zip(gens, w=None):
            gens = list(gens)
            w = list(w) if w else [1] * len(gens)
            while gens:
                for g_, k_ in list(zip(gens, w)):
                    try:
                        for _ in range(k_):
                            next(g_)
                    except StopIteration:
                        i_ = gens.index(g_)
                        gens.pop(i_)
                        w.pop(i_)

        for h in range(0 if "att" in M0SKIP else 8):
            dma("sp", qT, pt_scr[h], t_pt[h], [t_q])
            dma("sp", kT, pt_scr[8 + h], t_pt[8 + h], [t_k])
            dma("sp", vT, pt_scr[16 + h], t_pt[16 + h], [t_v])
            tb = TB[h % 2]
            ttb = tTB[h % 2]
            skew = bass.AP(tensor=bs_scr.tensor, offset=h * 128 * 383 + 127, ap=[[382, 128], [1, 256]])
            dma("sp", tb, skew, [t_bs[h]], [ttb])
            for half in range(2):
                pv_, tpv_ = nps()
                pvb = pv_[:].bitcast(BF16)
                for i in range(8):
                    tr(pvb[:, i * 128:(i + 1) * 128], vT[:, (half * 8 + i) * 128:(half * 8 + i + 1) * 128], ident_b,
                       [t_v, t_cst], [tpv_])
                cp("dve", vtm[:, half * 8:(half + 1) * 8, :], pvb.rearrange("p (i d) -> p i d", i=8), [tpv_], [t_vtm])
            for it in range(18):
                chains = []
                wts = []
                if it < 16:
                    chains.append(att_A(h, it, tb, ttb, 0))
                    wts.append(1)
                    chains.append(att_A(h, it, tb, ttb, 1))
                    wts.append(1)
                if 0 <= it - 1 < 16:
                    chains.append(att_B1(h, it - 1))
                    wts.append(1)
                if 0 <= it - 2 < 16:
                    chains.append(att_B2(h, it - 2))
                    wts.append(1)
                run_zip(chains, wts)
            out_proj(h * 128, yaT, t_yaT, 0)
        P.barrier()

        P.scope = "m0ml"
        carve_reset()
        sm = carve(192, F32).rearrange("p (a b) -> p a b", a=3)
        qraw = carve(S + 4, BF16)
        kraw = carve(S + 4, BF16)
        t_qr, t_kr = Tok("qraw"), Tok("kraw")
        cacc = carve(S, F32)
        t_cacc = Tok("cacc")
        qc = carve(S, BF16)
        kc = carve(S, BF16)
        t_qc, t_kc = Tok("qc"), Tok("kc")
        ktm = carve(S, BF16).rearrange("p (i d) -> p i d", i=16)
        t_ktm = Tok("ktm")
        vT2 = [carve(S, BF16) for _ in range(2)]
        t_v2 = [Tok("vT2a"), Tok("vT2b")]
        v1 = carve(16 * 258, BF16).rearrange("p (i d) -> p i d", i=16)
        t_v1 = Tok("v1")
        oT = [carve(S, BF16) for _ in range(2)]
        t_oT = Tok("oT")
        Cst = carve(258, F32)
        Cb2 = [carve(258, BF16) for _ in range(2)]
        t_C = Tok("C")
        t_Cb2 = [Tok("Cb0"), Tok("Cb1")]
        Pm2 = [carve(128, BF16) for _ in range(2)]
        t_Pm2 = [Tok("Pm0"), Tok("Pm1")]
        va = [carve(258, BF16) for _ in range(2)]
        tva = [Tok("va0"), Tok("va1")]
        hh = carve(256, F32)
        t_hh = Tok("hh")
        hn = carve(256, BF16)
        t_hn = Tok("hn")
        ybT = vT2
        t_yb = t_v2
        junk = carve(256, F32)
        t_junk = Tok("junk")
        st8 = carve(16, F32)
        t_st = Tok("st")
        wo = [carve(S, BF16) for _ in range(1)]
        two = [Tok("wo0")]
        mng = carve(1024, F32)
        t_mng = Tok("mng")
        cw = carve(32, F32).rearrange("p (c k) -> p c k", c=8)
        cbv = carve(8, F32)
        t_cw = Tok("cw")
        dma("sp", mng, mnormg, [], [t_mng])
        dma("sp", cw.rearrange("p c k -> p (c k)"), mconvw, [], [t_cw])
        dma("sp", cbv, mconvb, [], [t_cw])
        epsc = carve(2, F32)[:, 0:1]
        t_eps = Tok("eps")
        P.op("dve", lambda e, o=epsc: e.memset(o, 1e-6), writes=[t_eps])
        P.op("dve", lambda e, o=qraw[:, 0:3]: e.memset(o, 0.0), writes=[t_qr])
        P.op("dve", lambda e, o=kraw[:, 0:3]: e.memset(o, 0.0), writes=[t_kr])
        P.op("dve", lambda e, o=v1[:, :, 256:257]: e.memset(o, 1.0), writes=[t_v1])

        def conv4(raw, t_raw, ci, dst, t_dst, qscale):
            ts("dve", cacc, raw[:, 0:S], cw[:, ci, 0:1], cbv[:, ci:ci + 1], ALU.mult, ALU.add,
               [t_raw, t_cw], [t_cacc])
            for k in range(1, 4):
                stt("dve", cacc, raw[:, k:k + S], cw[:, ci, k:k + 1], cacc, ALU.mult, ALU.add,
                    [t_raw, t_cw, t_cacc], [t_cacc])
            if qscale is None:
                act(dst, cacc, AF.Silu, [t_cacc], [t_dst])
            else:
                act(cacc, cacc, AF.Silu, [t_cacc], [t_cacc])
                ts("dve", dst, cacc, qscale, None, ALU.mult, None, [t_cacc], [t_dst])

        for h in range(0 if "ml" in M0SKIP else 4):
            dma("sp", qraw[:, 3:3 + S], pt_scr[24 + h], t_pt[24 + h], [t_qr])
            dma("sp", kraw[:, 3:3 + S], pt_scr[28 + h], t_pt[28 + h], [t_kr])
            for e_ in range(2):
                dma("sp", vT2[e_], pt_scr[32 + 2 * h + e_], t_pt[32 + 2 * h + e_], [t_v2[e_]])
                dma("sp", oT[e_], pt_scr[40 + 2 * h + e_], t_pt[40 + 2 * h + e_], [t_oT])
            conv4(qraw, t_qr, h, qc, t_qc, 128.0 ** -0.5)
            conv4(kraw, t_kr, 4 + h, kc, t_kc, None)
            for e_ in range(2):
                act(oT[e_], oT[e_], AF.Sigmoid, [t_oT], [t_oT])
            for half in range(2):
                pk_, tpk_ = nps()
                pkb = pk_[:].bitcast(BF16)
                for i in range(8):
                    ii = half * 8 + i
                    tr(pkb[:, i * 128:(i + 1) * 128], kc[:, ii * 128:(ii + 1) * 128], ident_b, [t_kc, t_cst], [tpk_])
                cp("dve", ktm[:, half * 8:(half + 1) * 8, :], pkb.rearrange("p (i d) -> p i d", i=8), [tpk_], [t_ktm])
                for e_ in range(2):
                    pv_, tpv_ = nps()
                    pvb = pv_[:].bitcast(BF16)
                    for i in range(8):
                        ii = half * 8 + i
                        tr(pvb[:, i * 128:(i + 1) * 128], vT2[e_][:, ii * 128:(ii + 1) * 128], ident_b,
                           [t_v2[e_], t_cst], [tpv_])
                    cp("act", v1[:, half * 8:(half + 1) * 8, e_ * 128:(e_ + 1) * 128],
                       pvb.rearrange("p (i d) -> p i d", i=8), [tpv_], [t_v1])
            P.op("dve", lambda e, o=Cst[:, 0:257]: e.memset(o, 0.0), writes=[t_C])
            P.op("dve", lambda e, o=Cb2[0][:, 0:257]: e.memset(o, 0.0), writes=[t_Cb2[0]])
            pos = {}

            def ml_front(c):
                sl = slice(c * 128, (c + 1) * 128)
                col = c * 4 + h
                vb_ = va[c % 2]
                tvb = tva[c % 2]
                Pm_ = Pm2[c % 2]
                tPm = t_Pm2[c % 2]
                ps_, tps_ = nps()
                mm(ps_[:, 0:128], kc[:, sl], qc[:, sl], True, True, [t_kc, t_qc], [tps_])
                yield
                act(vb_[:, 0:257], v1[:, c, 0:257], AF.Copy, [t_v1, t_sm], [tvb], scale=sm[:, 1, col:col + 1])
                tt("dve", Pm_, ps_[:, 0:128], tri_b, ALU.mult, [tps_, t_cst], [tPm])
                yield
                pc_, tpc_ = nps(hold=True)
                mm(pc_[:, 0:257], ktm[:, c, :], vb_[:, 0:257], True, True, [t_ktm, tvb], [tpc_])
                yield
                po_, tpo_ = nps(hold=True)
                pos[c] = (po_, tpo_)
                mm(po_[:, 0:257], Pm_, vb_[:, 0:257], True, False, [tPm, tvb], [tpo_])
                mm(po_[:, 0:257], qc[:, sl], Cb2[c % 2][:, 0:257], False, True, [t_qc, t_Cb2[c % 2]], [tpo_])
                yield
                tt("dve", Cst[:, 0:257], Cst[:, 0:257], pc_[:, 0:257], ALU.add, [t_C, tpc_], [t_C])
                rel(pc_)
                yield
                ts("dve", Cst[:, 0:257], Cst[:, 0:257], sm[:, 2, col:col + 1], None, ALU.mult, None, [t_C, t_sm], [t_C])
                yield
                cp("dve", Cb2[(c + 1) % 2][:, 0:257], Cst[:, 0:257], [t_C], [t_Cb2[(c + 1) % 2]])
                yield

            def ml_back(c):
                sl = slice(c * 128, (c + 1) * 128)
                col = c * 4 + h
                po_, tpo_ = pos.pop(c)
                act(st8[:, 0:1], po_[:, 256:257], AF.Abs, [tpo_], [t_st])
                yield
                ts("dve", st8[:, 0:1], st8[:, 0:1], sm[:, 0, col:col + 1], 1.0, ALU.mult, ALU.max,
                   [t_st, t_sm], [t_st])
                yield
                P.op("dve", lambda e, o=st8[:, 1:2], i=st8[:, 0:1]: e.reciprocal(o, i), reads=[t_st], writes=[t_st])
                yield
                tt("dve", st8[:, 2:3], st8[:, 1:2], sm[:, 0, col:col + 1], ALU.mult, [t_st, t_sm], [t_st])
                yield
                ts("dve", hh, po_[:, 0:256], st8[:, 2:3], None, ALU.mult, None, [tpo_, t_st], [t_hh])
                yield
                act(junk, hh, AF.Square, [t_hh], [t_junk, t_st], accum=st8[:, 3:4])
                yield
                act(st8[:, 4:5], st8[:, 3:4], AF.Ln, [t_st, t_eps], [t_st], bias=epsc, scale=1.0 / 256)
                yield
                act(st8[:, 5:6], st8[:, 4:5], AF.Exp, [t_st], [t_st], scale=-0.5)
                yield
                stt("dve", hn, hh, st8[:, 5:6], mng[:, h * 256:(h + 1) * 256], ALU.mult, ALU.mult,
                    [t_hh, t_st, t_mng], [t_hn])
                yield
                for e_ in range(2):
                    py_, tpy_ = nps()
                    pyb = py_[:].bitcast(BF16)
                    tr(pyb[:, 0:128], hn[:, e_ * 128:(e_ + 1) * 128], ident_b, [t_hn, t_cst], [tpy_])
                    yield
                    tt("dve", ybT[e_][:, sl], pyb[:, 0:128], oT[e_][:, sl], ALU.mult, [tpy_, t_oT], [t_yb[e_]])
                    if e_ == 1:
                        rel(po_)
                    yield

            def run_zip2(gens):
                gens = list(gens)
                while gens:
                    for g_ in list(gens):
                        try:
                            next(g_)
                        except StopIteration:
                            gens.remove(g_)

            for it in range(17):
                chains = []
                if it < 16:
                    chains.append(ml_front(it))
                if it >= 1:
                    chains.append(ml_back(it - 1))
                run_zip2(chains)
            for e_ in range(2):
                out_proj(1024 + h * 256 + e_ * 128, ybT[e_], t_yb[e_], 0)
        P.barrier()

        P.scope = "m0out"
        carve_reset()
        yy = [carve(16 * 512, BF16).rearrange("p (c t) -> p c t", c=16) for _ in range(2)]
        t_yy = [Tok("yy0"), Tok("yy1")]
        wob = [carve(2048, BF16).rearrange("p (k n) -> p k n", k=16) for _ in range(3)]
        twob = [Tok("wob%d" % i) for i in range(3)]
        WO = mix_w_out.rearrange("(k p) n -> p k n", p=128)
        nw = 0
        for j in range(4):
            yb_ = yy[j % 2]
            dma("sp", yb_, ya_scr[:, :, j * 512:(j + 1) * 512].rearrange("c p t -> p c t"),
                [t_ya[ci] for ci in range(16) if t_ya[ci].w is not None], [t_yy[j % 2]])
            for dc in range(16):
                ws_ = nw % 3
                nw += 1
                dma("pool", wob[ws_], WO[:, :, dc * 128:(dc + 1) * 128], [], [twob[ws_]])
                po_, tpo_ = nps()
                for k in range(16):
                    mm(po_[:], wob[ws_][:, k, :], yb_[:, k, :], k == 0, k == 15, [twob[ws_], t_yy[j % 2]], [tpo_])
                xs = xT[:, dc, j * 512:(j + 1) * 512]
                stt("dve", xs, po_[:], par[:, 2, dc:dc + 1], xs, ALU.mult, ALU.add,
                    [tpo_, t_par, tx[dc][j]], [tx[dc][j]])
        P.barrier()

    if L0:
        compute_mod(0, 0, 96, reset=False)
        if "ffn0a" in STAGES:
            ffn(0, 0)
        if "mix0" in STAGES:
            mixer0(0)
        compute_mod(0, 96, 144)
        if "ffn0b" in STAGES:
            ffn(0, 1)
    if L1:
        compute_mod(1)
        if "ffn1a" in STAGES:
            ffn(1, 0)
        if "mix1" in STAGES:
            mixer1(1)
        if "ffn1b" in STAGES:
            ffn(1, 1)
    final_phase()
    P.emit(st)
    st.close()
    return nc


_NC = None


def _prep_inputs(inp):
    f32 = np.float32
    g = lambda k: np.ascontiguousarray(np.asarray(inp[k], dtype=f32))
    vlist = []
    ng = g("norm_g")
    for l in range(2):
        for s in range(3):
            vlist.append(ng[l, s])
    vlist.append(g("final_g"))
    vlist.append(g("conv_dw_b")[0])
    vlist.append(g("conv_ln_g")[0])
    vlist.append(g("conv_ln_b")[0])
    vlist.append(g("conv_pw2_b")[0])
    pb = g("conv_pw1_b")[0]
    vlist.append(pb[:D])
    vlist.append(pb[D:])
    dw = g("conv_dw_w")[0]
    for k in range(31):
        vlist.append(dw[k])
    vecs = np.stack([v.reshape(16, 128).T for v in vlist], axis=1)
    vecs = np.ascontiguousarray(vecs.reshape(128, 44 * 16))
    mb = g("mod_b")
    modb = np.ascontiguousarray(np.stack([mb[l].reshape(144, 128).T for l in range(2)], axis=1).reshape(128, 288))
    cst = np.zeros((128, 384), f32)
    cst[:, 0:128] = np.eye(128, dtype=f32)
    cst[:, 128:256] = np.triu(np.ones((128, 128), f32))
    cst[:, 256:384] = 1.0
    rel = g("rel_table")
    relx = np.zeros((33, 8, 128), f32)
    relx[:32] = rel[:, :, None]
    relx[32] = NEG
    lam = g("diff_lambda")[0].reshape(1, 256)
    mcw = g("mlstm_conv_w")[0]
    mconvw = np.ascontiguousarray(mcw.reshape(4, 8, 128).transpose(2, 1, 0).reshape(128, 32))
    mconvb = np.ascontiguousarray(g("mlstm_conv_b")[0].reshape(8, 128).T)
    USED.update(("x", "cvec"))
    shared = {
        "vecs": vecs, "modb": modb, "cst": cst,
        "mod_w0": g("mod_w")[0], "mod_w1": g("mod_w")[1], "ffn_w1": g("ffn_w1"), "ffn_w3": g("ffn_w3"), "ffn_w2": g("ffn_w2"),
        "mix_w_in": g("mix_w_in")[0], "mix_w_out": g("mix_w_out")[0],
        "pw1_w": g("conv_pw1_w")[0], "pw2_w": g("conv_pw2_w")[0],
        "relx": np.ascontiguousarray(relx.reshape(33, 1024)), "oh": _bucket_table(),
        "lamrep": np.ascontiguousarray(np.broadcast_to(lam, (128, 256))),
        "sublng": np.ascontiguousarray(np.broadcast_to(g("diff_subln_g")[0][None, :], (128, 128))),
        "mnormg": np.ascontiguousarray(np.broadcast_to(g("mlstm_norm_g")[0][None, :], (128, 1024))),
        "gateb": np.ascontiguousarray(np.broadcast_to(g("mlstm_gate_b")[0].reshape(1, 8), (128, 8))),
        "mconvw": mconvw, "mconvb": mconvb,
    }
    x = g("x")
    c = g("c")
    maps = []
    for b in range(NCORES):
        m = dict(shared)
        m["x"] = x[b]
        m["cvec"] = np.ascontiguousarray(c[b].reshape(16, 128).T)
        maps.append(m)
    return maps


def kernel(**inputs):
    global _NC
    nc = build()
    maps = _prep_inputs(inputs)
    maps = [{k: v for k, v in m.items() if k in USED} for m in maps]
    res = run_bass_kernel_spmd(nc, maps, core_ids=list(range(NCORES)))
    if DEBUG:
        LAST.update(res.results[0])
    return np.stack([r["out"] for r in res.results], axis=0).astype(np.float32)
```
